# Optimizing a Trainium2 kernel written in Bass

```python
import jax, jax.numpy as jnp
from jax import lax
import numpy as np

D_MODEL = 1024
BATCH = 2
SEQ = 16384
DEPTH = 1
DEC_BATCH = 8
DEC_SEQ = 2048
PAST_LEN = 128

N_FOURIER_GROUPS = 4
FOURIER_GROUP_DIM = D_MODEL // 8
FOURIER_WIDTH = N_FOURIER_GROUPS * FOURIER_GROUP_DIM
POOL_WINDOWS = (2, 4, 8, 16)
N_POOL_GROUPS = len(POOL_WINDOWS)
POOL_GROUP_DIM = D_MODEL // 8
POOL_WIDTH = N_POOL_GROUPS * POOL_GROUP_DIM
N_BRANCHES = 2
IN_WIDTH = FOURIER_WIDTH + POOL_WIDTH + N_BRANCHES * D_MODEL
D_FF = ((8 * D_MODEL // 3 + 255) // 256) * 256
N_ADA = 6
EPS = 1e-6

kernel_name = "fourier_pool_hybrid_encoder"


def _rmsnorm(x, g):
    xf = x.astype(jnp.float32)
    y = xf * lax.rsqrt(jnp.mean(xf * xf, axis=-1, keepdims=True) + EPS)
    return (y * g.astype(jnp.float32)).astype(x.dtype)


def _modulate(h, shift, scale):
    return h * (1 + scale[:, None, :]) + shift[:, None, :]


def _fourier_mix(u):
    b, s, _ = u.shape
    ug = u.reshape(b, s, N_FOURIER_GROUPS, FOURIER_GROUP_DIM).astype(jnp.float32)
    f = jnp.fft.fft2(ug, axes=(1, 3), norm="ortho").real
    return f.reshape(b, s, FOURIER_WIDTH).astype(u.dtype)


def _centred_mean(u, w):
    s = u.shape[1]
    left = w // 2
    right = w - 1 - left
    up = jnp.pad(u.astype(jnp.float32), ((0, 0), (left, right), (0, 0)))
    cs = jnp.pad(jnp.cumsum(up, axis=1), ((0, 0), (1, 0), (0, 0)))
    win = cs[:, w:w + s] - cs[:, :s]
    t = jnp.arange(s)
    cnt = jnp.minimum(t + right, s - 1) - jnp.maximum(t - left, 0) + 1
    return win / cnt.astype(jnp.float32)[None, :, None]


def _pool_mix(u, w_pg, pool_scale):
    b, s, _ = u.shape
    ug = u.reshape(b, s, N_POOL_GROUPS, POOL_GROUP_DIM)
    diffs = [_centred_mean(ug[:, :, i], POOL_WINDOWS[i]) - ug[:, :, i].astype(jnp.float32)
             for i in range(N_POOL_GROUPS)]
    d = jnp.stack(diffs, axis=2).astype(u.dtype)
    y = jnp.einsum("bsgc,gcd->bsgd", d, w_pg)
    return y.reshape(b, s, POOL_WIDTH) * pool_scale


def _layer(x, c, w_ada, b_ada, g_pre_mix, w_in, w_fo, w_pg, pool_scale, w_po, w_out,
           g_post_mix, g_pre_ffn, w_gate, w_up, w_down, g_post_ffn):
    ada = jnp.einsum("bd,de->be", jax.nn.silu(c), w_ada) + b_ada
    sh1, sc1, gt1, sh2, sc2, gt2 = jnp.split(ada, N_ADA, axis=-1)

    h = _modulate(_rmsnorm(x, g_pre_mix), sh1, sc1)
    proj = jnp.einsum("bsd,de->bse", h, w_in)
    u_f = proj[..., :FOURIER_WIDTH]
    u_p = proj[..., FOURIER_WIDTH:FOURIER_WIDTH + POOL_WIDTH]
    gate_logits = proj[..., FOURIER_WIDTH + POOL_WIDTH:]
    y_f = jnp.einsum("bsf,fd->bsd", _fourier_mix(u_f), w_fo)
    y_p = jnp.einsum("bsp,pd->bsd", _pool_mix(u_p, w_pg, pool_scale), w_po)
    gates = jax.nn.sigmoid(gate_logits)
    g_f = gates[..., :D_MODEL]
    g_p = gates[..., D_MODEL:]
    m = jnp.einsum("bsd,de->bse", g_f * y_f + g_p * y_p, w_out)
    x = x + gt1[:, None, :] * _rmsnorm(m, g_post_mix)

    h = _modulate(_rmsnorm(x, g_pre_ffn), sh2, sc2)
    a = jnp.einsum("bsd,df->bsf", h, w_gate)
    bup = jnp.einsum("bsd,df->bsf", h, w_up)
    z = jnp.einsum("bsf,fd->bsd", jax.nn.silu(a) * bup, w_down)
    x = x + gt2[:, None, :] * _rmsnorm(z, g_post_ffn)
    return x


def setup_inputs(seed: int = 0) -> dict:
    key = jax.random.key(seed)
    ks = jax.random.split(key, 20)
    D = D_MODEL

    def nrm(k, shape, fan_in):
        return jax.random.normal(k, shape, jnp.float32) * (fan_in ** -0.5)

    def gain(k, shape):
        return 1.0 + 0.1 * jax.random.normal(k, shape, jnp.float32)

    return {
        "x_prompt": jax.random.normal(ks[0], (BATCH, SEQ, D), jnp.float32),
        "x_sample": jax.random.normal(ks[1], (DEC_BATCH, DEC_SEQ, D), jnp.float32),
        "c_prompt": jax.random.normal(ks[2], (BATCH, D), jnp.float32),
        "c_sample": jax.random.normal(ks[3], (DEC_BATCH, D), jnp.float32),
        "w_ada": nrm(ks[4], (DEPTH, D, N_ADA * D), D),
        "b_ada": 0.01 * jax.random.normal(ks[5], (DEPTH, N_ADA * D), jnp.float32),
        "g_pre_mix": gain(ks[6], (DEPTH, D)),
        "w_in": nrm(ks[7], (DEPTH, D, IN_WIDTH), D),
        "w_fo": nrm(ks[8], (DEPTH, FOURIER_WIDTH, D), FOURIER_WIDTH),
        "w_pg": nrm(ks[9], (DEPTH, N_POOL_GROUPS, POOL_GROUP_DIM, POOL_GROUP_DIM), POOL_GROUP_DIM),
        "pool_scale": gain(ks[10], (DEPTH, POOL_WIDTH)),
        "w_po": nrm(ks[11], (DEPTH, POOL_WIDTH, D), POOL_WIDTH),
        "w_out": nrm(ks[12], (DEPTH, D, D), D),
        "g_post_mix": gain(ks[13], (DEPTH, D)),
        "g_pre_ffn": gain(ks[14], (DEPTH, D)),
        "w_gate": nrm(ks[15], (DEPTH, D, D_FF), D),
        "w_up": nrm(ks[16], (DEPTH, D, D_FF), D),
        "w_down": nrm(ks[17], (DEPTH, D_FF, D), D_FF),
        "g_post_ffn": gain(ks[18], (DEPTH, D)),
    }


def reference(x_prompt, x_sample, c_prompt, c_sample, w_ada, b_ada, g_pre_mix, w_in, w_fo,
              w_pg, pool_scale, w_po, w_out, g_post_mix, g_pre_ffn, w_gate, w_up, w_down,
              g_post_ffn):
    y_prompt = x_prompt
    y_sample = x_sample
    for l in range(DEPTH):
        params = (w_ada[l], b_ada[l], g_pre_mix[l], w_in[l], w_fo[l], w_pg[l], pool_scale[l],
                  w_po[l], w_out[l], g_post_mix[l], g_pre_ffn[l], w_gate[l], w_up[l],
                  w_down[l], g_post_ffn[l])
        y_prompt = _layer(y_prompt, c_prompt, *params)
        y_sample = _layer(y_sample, c_sample, *params)
    return (y_prompt, y_sample)
```

```python
from contextlib import ExitStack
import numpy as np
import ml_dtypes
import concourse.bass as bass
import concourse.mybir as mybir
from concourse.bass_utils import run_bass_kernel_spmd

F32 = mybir.dt.float32
BF16 = mybir.dt.bfloat16
ALU = mybir.AluOpType
AF = mybir.ActivationFunctionType
BF = ml_dtypes.bfloat16

D = 1024
SP = 16384
SS = 2048
DFF = 2816
NJ = 22
NRUN = 192
NOWN = 6144
NMT = 12
EPS = 1e-6
WINS = (2, 4, 8, 16)


class Buf:
    __slots__ = ("w", "r", "prev")

    def __init__(self):
        self.w = []
        self.r = []
        self.prev = []


class Sched:
    ENG = ("pe", "act", "dve", "pool", "sp")

    def __init__(self, nc):
        self.nc = nc
        self.q = {e: [] for e in self.ENG}
        self.cnt = {e: 0 for e in self.ENG}
        self.sem = {e: nc.alloc_semaphore(name="ms_" + e) for e in self.ENG}
        self.seen = {e: {} for e in self.ENG}
        self.dma_sems = [nc.alloc_semaphore(name="dq%d" % i) for i in range(32)]
        self.dma_cnt = [0] * len(self.dma_sems)
        self.dma_rr = 0

    @staticmethod
    def _merge(toks):
        d = {}
        for k, v in toks:
            if d.get(k, 0) < v:
                d[k] = v
        return list(d.items())

    def _deps(self, reads, writes, partial):
        deps = []
        for b in reads:
            deps += b.w
        for b in writes:
            if partial and not b.r:
                deps += b.prev
            else:
                b.prev = self._merge(b.w + b.r)
                b.w = []
                b.r = []
                deps += b.prev
        return self._merge(deps)

    def _filter(self, eng, deps):
        out = []
        seen = self.seen[eng]
        for k, v in deps:
            if eng == "pe" and k == "pe":
                continue
            if seen.get(k, 0) >= v:
                continue
            seen[k] = v
            out.append((k, v))
        return out

    def op(self, eng, fn, reads=(), writes=(), partial=False):
        deps = self._filter(eng, self._deps(reads, writes, partial))
        self.cnt[eng] += 1
        tok = (eng, self.cnt[eng])
        for b in reads:
            b.r = self._merge(b.r + [tok])
        for b in writes:
            b.w = self._merge(b.w + [tok])
        self.q[eng].append((fn, deps, None))
        return tok

    def dma(self, eng, out, in_, reads=(), writes=(), partial=False):
        i = self.dma_rr
        self.dma_rr = (self.dma_rr + 1) % len(self.dma_sems)
        deps = self._deps(reads, writes, partial)
        if self.dma_cnt[i] > 0:
            deps = self._merge(deps + [(("dma", i), self.dma_cnt[i])])
        deps = self._filter(eng, deps)
        self.dma_cnt[i] += 16
        tok = (("dma", i), self.dma_cnt[i])
        for b in reads:
            b.r = self._merge(b.r + [tok])
        for b in writes:
            b.w = self._merge(b.w + [tok])
        self.q[eng].append((lambda e: e.dma_start(out=out, in_=in_), deps, i))
        return tok

    def barrier(self):
        toks = [(e, self.cnt[e]) for e in self.ENG if self.cnt[e] > 0]
        toks += [(("dma", i), c) for i, c in enumerate(self.dma_cnt) if c > 0]
        for e in self.ENG:
            deps = self._filter(e, list(toks))
            self.q[e].append((None, deps, None))

    def _semof(self, k):
        if isinstance(k, tuple):
            return self.dma_sems[k[1]]
        return self.sem[k]

    def emit(self):
        def run(eng_name):
            def body(e):
                for fn, deps, dsem in self.q[eng_name]:
                    for k, v in deps:
                        e.wait_ge(self._semof(k), v)
                    if fn is None:
                        continue
                    ins = fn(e)
                    if dsem is None:
                        ins.then_inc(self.sem[eng_name], 1)
                    else:
                        ins.then_inc(self.dma_sems[dsem], 16)
            return body

        with self.nc.Block() as block:
            block.tensor(run("pe"))
            block.scalar(run("act"))
            block.vector(run("dve"))
            block.gpsimd(run("pool"))
            block.sync(run("sp"))
        self.q = {e: [] for e in self.ENG}


def fourier_consts(q):
    s2 = np.arange(128)
    k2 = 32 * q + np.arange(32)
    ang = 2 * np.pi * ((np.outer(s2, k2)) % 128) / 128.0
    E_p = np.empty((128, 2, 32))
    E_p[:, 0, :] = np.cos(ang)
    E_p[:, 1, :] = -np.sin(ang)
    E_p = E_p.reshape(128, 64)
    s1 = np.arange(128)
    k1 = np.arange(128)
    T_p = np.zeros((32, 128, 2, 256))
    for g in range(32):
        k = 128 * k1 + k2[g]
        th = 2 * np.pi * ((np.outer(s1, k)) % SP) / float(SP)
        Tr, Ti = np.cos(th), -np.sin(th)
        T_p[g, :, 0, :128] = Tr
        T_p[g, :, 0, 128:] = Ti
        T_p[g, :, 1, :128] = -Ti
        T_p[g, :, 1, 128:] = Tr
    ang = 2 * np.pi * ((np.outer(s2, np.arange(128))) % 128) / 128.0
    E_s = np.empty((128, 128, 2))
    E_s[..., 0] = np.cos(ang)
    E_s[..., 1] = -np.sin(ang)
    E_s = E_s.reshape(128, 256)
    T_s = np.zeros((16, 128, 2, 256))
    for B in range(16):
        for s1v in range(16):
            for ks in range(8):
                r = s1v * 8 + ks
                k1v = np.arange(16)
                k = 128 * k1v + 8 * B + ks
                th = 2 * np.pi * ((k * s1v) % SS) / float(SS)
                Tr, Ti = np.cos(th), -np.sin(th)
                col = k1v * 8 + ks
                T_s[B, r, 0, col] = Tr
                T_s[B, r, 0, 128 + col] = Ti
                T_s[B, r, 1, col] = -Ti
                T_s[B, r, 1, 128 + col] = Tr
    c = np.arange(128)
    angc = 2 * np.pi * ((np.outer(c, c)) % 128) / 128.0
    CS = np.stack([np.cos(angc), np.sin(angc)], axis=1)
    return (E_p.astype(BF), T_p.astype(BF), E_s.astype(BF), T_s.astype(BF), CS.astype(BF))


def run_starts(q):
    st = [(0, 128 * k1 + 32 * q, SP) for k1 in range(128)]
    st += [(1, 32 * r, SS) for r in range(64)]
    return st


def inv_counts(q):
    icn = np.zeros((4, NOWN), np.float32)
    for r, (_, start, S) in enumerate(run_starts(q)):
        t = start + np.arange(32)
        for g, w in enumerate(WINS):
            left = w // 2
            right = w - 1 - left
            cnt = np.minimum(t + right, S - 1) - np.maximum(t - left, 0) + 1
            icn[g, 32 * r:32 * r + 32] = 1.0 / cnt.astype(np.float32)
    return icn


def build_program():
    nc = bass.Bass("TRN2", target_bir_lowering=False)

    def din(name, shape, dt=F32):
        return nc.dram_tensor(name, list(shape), dt, kind="ExternalInput").ap()

    xa = din("xa", [SP + SS, D])
    xb = din("xb", [NRUN * 48, D])
    msk_d = din("msk", [128, NRUN * 48])
    icn_d = din("icn", [128, 4, NOWN])
    colv_d = din("colv", [128, 70])
    rowv_d = din("rowv", [128, 4, D])
    ident_d = din("ident", [128, 128], BF16)
    Ep_d = din("Ep", [128, 64], BF16)
    Tp_d = din("Tp", [32, 128, 2, 256], BF16)
    Es_d = din("Es", [128, 256], BF16)
    Ts_d = din("Ts", [16, 128, 2, 256], BF16)
    CS_d = din("CS", [128, 2, 128], BF16)
    w_ada = din("w_ada", [D, 6 * D])
    w_in = din("w_in", [D, 3072])
    w_fo = din("w_fo", [512, D])
    w_pg = din("w_pg", [4, 128, 128])
    w_po = din("w_po", [512, D])
    w_out = din("w_out", [D, D])
    w_gate = din("w_gate", [D, DFF])
    w_up = din("w_up", [D, DFF])
    w_down = din("w_down", [DFF, D])
    y_d = nc.dram_tensor("y", [NOWN, D], F32, kind="ExternalOutput").ap()
    zf_d = nc.dram_tensor("zf_scr", [4, 128, NOWN], BF16).ap()
    x1_d = nc.dram_tensor("x1_scr", [NOWN, D], F32).ap()
    wg_bf = nc.dram_tensor("wg_bf", [D, DFF], BF16).ap()
    wu_bf = nc.dram_tensor("wu_bf", [D, DFF], BF16).ap()
    wd_bf = nc.dram_tensor("wd_bf", [DFF, D], BF16).ap()
    win_bf = nc.dram_tensor("win_bf", [D, 2560], BF16).ap()
    wfo_bf = nc.dram_tensor("wfo_bf", [512, D], BF16).ap()
    wpo_bf = nc.dram_tensor("wpo_bf", [512, D], BF16).ap()
    wout_bf = nc.dram_tensor("wout_bf", [D, D], BF16).ap()
    wpg_bf = nc.dram_tensor("wpg_bf", [4, 128, 128], BF16).ap()

    S = Sched(nc)
    top = ExitStack()

    def sb(es, name, shape, dt):
        return es.enter_context(nc.sbuf_tensor("s_" + name, list(shape), dt))

    def ps(es, name, shape, dt):
        return es.enter_context(nc.psum_tensor("p_" + name, list(shape), dt))

    ident = sb(top, "ident", [128, 128], BF16)
    colv = sb(top, "colv", [128, 70], F32)
    acol = sb(top, "acol", [128, 2, 4, 8], F32)
    rowg2 = sb(top, "rowg2", [128, 2, D], F32)
    mid = ExitStack()
    rowg1 = sb(mid, "rowg1", [128, 2, D], F32)
    rowg_l = [rowg1, rowg2]
    b_ident, b_colv, b_acol, b_rowg = Buf(), Buf(), Buf(), Buf()
    b_zf, b_x1 = Buf(), Buf()
    b_wscr = Buf()
    for dst_, src_ in ((win_bf, w_in[:, 512:3072]), (wfo_bf, w_fo), (wpo_bf, w_po), (wout_bf, w_out), (wpg_bf, w_pg),
                       (wg_bf, w_gate), (wu_bf, w_up), (wd_bf, w_down)):
        S.dma("pool", dst_, src_, writes=[b_wscr], partial=True)
    S.dma("sp", ident[:], ident_d, writes=[b_ident])
    S.dma("sp", colv[:], colv_d, writes=[b_colv])
    out_toks = []

    def act_rstd(dst, src, b_dst, b_src):
        S.op("act", lambda e: e.activation(out=dst, in_=src, func=AF.Ln, scale=1.0 / D, bias=EPS),
             reads=[b_src], writes=[b_dst])
        S.op("act", lambda e: e.activation(out=dst, in_=dst, func=AF.Exp, scale=-0.5),
             reads=[b_dst], writes=[b_dst])

    with ExitStack() as es:
        stg = [sb(es, "p0_stg%d" % i, [128, 8, 512], F32) for i in range(2)]
        wbf = [sb(es, "p0_wbf%d" % i, [128, 8, 512], BF16) for i in range(2)]
        b_stg = [Buf(), Buf()]
        b_wbf = [Buf(), Buf()]
        scb = sb(es, "p0_sc", [128, 8, 2], BF16)
        screp = sb(es, "p0_screp", [128, 2, 8, 128], BF16)
        rowv = sb(es, "p0_rowv", [128, 4, D], F32)
        adac = sb(es, "p0_adac", [128, 4, 8, 2], F32)
        b_scb, b_screp, b_rowv, b_adac = Buf(), Buf(), Buf(), Buf()
        pc = ps(es, "p0_pc", [128, 4, 2], F32)
        pr = [ps(es, "p0_pr%d" % i, [128, 512], F32) for i in range(2)]
        b_pc = Buf()
        b_pr = [Buf(), Buf()]
        S.dma("sp", rowv[:], rowv_d, writes=[b_rowv])
        for b in range(2):
            S.op("act", lambda e, b=b: e.activation(out=scb[:, :, b], in_=colv[:, 8 * b:8 * b + 8], func=AF.Silu),
                 reads=[b_colv], writes=[b_scb], partial=True)
        for b in range(2):
            for k in range(8):
                S.op("dve", lambda e, b=b, k=k: e.tensor_copy(
                    out=screp[:, b, k, :], in_=scb[:, k, b:b + 1].to_broadcast([128, 128])),
                    reads=[b_scb], writes=[b_screp], partial=True)
        for pc_i in range(12):
            i = pc_i % 2
            S.dma("sp", stg[i][:], w_ada[:, 512 * pc_i:512 * pc_i + 512].rearrange("(k p) c -> p k c", p=128),
                  writes=[b_stg[i]])
            S.op("dve" if i == 0 else "pool", lambda e, i=i: e.tensor_copy(out=wbf[i][:], in_=stg[i][:]),
                 reads=[b_stg[i]], writes=[b_wbf[i]])
            vec = pc_i // 2
            half = pc_i % 2
            if vec in (2, 5):
                gi = 0 if vec == 2 else 1
                for b in range(2):
                    for k in range(8):
                        S.op("pe", lambda e, b=b, k=k, i=i: e.matmul(
                            pr[b][:], lhsT=screp[:, b, k, :], rhs=wbf[i][:, k, :], start=(k == 0), stop=(k == 7)),
                            reads=[b_screp, b_wbf[i]], writes=[b_pr[b]], partial=True)
                    dst = rowg_l[gi][:, b, 512 * half:512 * half + 512]
                    S.op("dve", lambda e, b=b, dst=dst, gi=gi, half=half: e.tensor_tensor(
                        out=dst, in0=pr[b][:], in1=rowv[:, 2 + gi, 512 * half:512 * half + 512], op=ALU.add),
                        reads=[b_pr[b], b_rowv], writes=[b_rowg], partial=True)
                    S.op("dve", lambda e, dst=dst, gi=gi, half=half: e.tensor_tensor(
                        out=dst, in0=dst, in1=rowv[:, gi, 512 * half:512 * half + 512], op=ALU.mult),
                        reads=[b_rowv, b_rowg], writes=[b_rowg], partial=True)
            else:
                vi = {0: 0, 1: 1, 3: 2, 4: 3}[vec]
                for c in range(4):
                    for k in range(8):
                        S.op("pe", lambda e, c=c, k=k, i=i: e.matmul(
                            pc[:, c, :], lhsT=wbf[i][:, k, 128 * c:128 * c + 128], rhs=scb[:, k, :],
                            start=(k == 0), stop=(k == 7)),
                            reads=[b_scb, b_wbf[i]], writes=[b_pc], partial=True)
                S.op("dve", lambda e, vi=vi, half=half: e.tensor_copy(
                    out=adac[:, vi, 4 * half:4 * half + 4, :], in_=pc[:]),
                    reads=[b_pc], writes=[b_adac], partial=True)
        for b in range(2):
            for si, (gcol, shv, scv) in enumerate(((16, 0, 1), (24, 2, 3))):
                bsh = 32 + 16 * si
                bsc = 40 + 16 * si
                a_dst = acol[:, b, 2 * si, :]
                b_dst_ = acol[:, b, 2 * si + 1, :]
                S.op("dve", lambda e, a_dst=a_dst, scv=scv, b=b, bsc=bsc: e.tensor_tensor(
                    out=a_dst, in0=adac[:, scv, :, b], in1=colv[:, bsc:bsc + 8], op=ALU.add),
                    reads=[b_adac, b_colv], writes=[b_acol], partial=True)
                S.op("dve", lambda e, a_dst=a_dst, gcol=gcol: e.scalar_tensor_tensor(
                    out=a_dst, in0=a_dst, scalar=1.0, in1=colv[:, gcol:gcol + 8], op0=ALU.add, op1=ALU.mult),
                    reads=[b_acol, b_colv], writes=[b_acol], partial=True)
                S.op("dve", lambda e, b_dst_=b_dst_, shv=shv, b=b, bsh=bsh: e.tensor_tensor(
                    out=b_dst_, in0=adac[:, shv, :, b], in1=colv[:, bsh:bsh + 8], op=ALU.add),
                    reads=[b_adac, b_colv], writes=[b_acol], partial=True)
        S.barrier()
        S.emit()

    MAGIC = 0x5F3759DF
    I32 = mybir.dt.int32

    def pipeline(stages, n):
        ns = len(stages)
        for step in range(n + ns - 1):
            for si in reversed(range(ns)):
                t = step - si
                if 0 <= t < n:
                    stages[si](t)

    def dve_rstd(wk, b_wk, srcs, b_src, out, b_out):
        a, yv, tv = wk[:, 0:1], wk[:, 1:2], wk[:, 2:3]
        if len(srcs) == 2:
            S.op("dve", lambda e: e.tensor_tensor(out=a, in0=srcs[0], in1=srcs[1], op=ALU.add),
                 reads=[b_src], writes=[b_wk])
            S.op("dve", lambda e: e.tensor_scalar(out=a, in0=a, scalar1=1.0 / D, scalar2=EPS, op0=ALU.mult, op1=ALU.add),
                 reads=[b_wk], writes=[b_wk])
        else:
            S.op("dve", lambda e: e.tensor_scalar(out=a, in0=srcs[0], scalar1=1.0 / D, scalar2=EPS, op0=ALU.mult,
                                                  op1=ALU.add), reads=[b_src], writes=[b_wk])
        S.op("dve", lambda e: e.tensor_single_scalar(out=yv.bitcast(I32), in_=a.bitcast(I32), scalar=1,
                                                     op=ALU.arith_shift_right), reads=[b_wk], writes=[b_wk])
        S.op("dve", lambda e: e.tensor_scalar(out=yv.bitcast(I32), in0=yv.bitcast(I32), scalar1=-1, scalar2=MAGIC,
                                              op0=ALU.mult, op1=ALU.add), reads=[b_wk], writes=[b_wk])
        for it in range(2):
            S.op("dve", lambda e: e.scalar_tensor_tensor(out=tv, in0=yv, scalar=a, in1=yv, op0=ALU.mult, op1=ALU.mult),
                 reads=[b_wk], writes=[b_wk])
            S.op("dve", lambda e: e.tensor_scalar(out=tv, in0=tv, scalar1=-0.5, scalar2=1.5, op0=ALU.mult, op1=ALU.add),
                 reads=[b_wk], writes=[b_wk])
            if it == 0:
                S.op("dve", lambda e: e.tensor_tensor(out=yv, in0=yv, in1=tv, op=ALU.mult), reads=[b_wk], writes=[b_wk])
            else:
                S.op("dve", lambda e: e.tensor_tensor(out=out, in0=yv, in1=tv, op=ALU.mult),
                     reads=[b_wk], writes=[b_out])

    def transposes(xn, b_xn, pT, b_pT):
        for k in range(8):
            S.op("pe", lambda e, k=k: e.transpose(out=pT[:, 128 * k:128 * k + 128], in_=xn[:, 128 * k:128 * k + 128],
                                                  identity=ident[:]),
                 reads=[b_xn, b_ident], writes=[b_pT], partial=True)

    def evacs(pT, b_pT, hT_dst, b_hT, bi, ni):
        for k in range(8):
            S.op("dve", lambda e, k=k: e.tensor_scalar(
                out=hT_dst(k), in0=pT[:, 128 * k:128 * k + 128],
                scalar1=acol[:, bi, 2 * ni, k:k + 1], scalar2=acol[:, bi, 2 * ni + 1, k:k + 1],
                op0=ALU.mult, op1=ALU.add),
                reads=[b_pT, b_acol], writes=[b_hT], partial=True)

    with ExitStack() as es:
        wf = sb(es, "a_wf", [128, 8, 512], BF16)
        wfa = [sb(es, "a_wfa%d" % i, [128, 8, 512], BF16) for i in range(2)]
        fixv = sb(es, "a_fixv", [128, 4, 2], F32)
        b_wf, b_fixv = Buf(), Buf()
        CSs = sb(es, "a_cs", [128, 2, 128], BF16)
        b_CS = Buf()
        S.dma("sp", CSs[:], CS_d, writes=[b_CS])
        with ExitStack() as es2:
            stg = sb(es2, "a_stg", [128, 8, 512], F32)
            b_st = Buf()
            S.dma("sp", stg[:], w_in[:, 0:512].rearrange("(k p) c -> p k c", p=128), writes=[b_st])
            S.op("dve", lambda e: e.tensor_copy(out=wf[:], in_=stg[:]), reads=[b_st], writes=[b_wf])
            bcol = sb(es2, "a_bcol", [128, 2, 8], BF16)
            b_bcol = Buf()
            S.op("dve", lambda e: e.tensor_copy(out=bcol[:], in_=acol[:, :, 1, :]), reads=[b_acol], writes=[b_bcol])
            pbw = ps(es2, "a_pbw", [128, 4, 2], F32)
            b_pbw = Buf()
            for b in range(2):
                for k in range(8):
                    S.op("dve", lambda e, b=b, k=k: e.tensor_scalar(
                        out=wfa[b][:, k, :], in0=stg[:, k, :], scalar1=acol[:, b, 0, k:k + 1], scalar2=None,
                        op0=ALU.mult), reads=[b_st, b_acol], writes=[b_wf], partial=True)
                for j in range(4):
                    for k in range(8):
                        S.op("pe", lambda e, b=b, j=j, k=k: e.matmul(
                            pbw[:, j, b:b + 1], lhsT=wf[:, k, 128 * j:128 * j + 128], rhs=bcol[:, b, k:k + 1],
                            start=(k == 0), stop=(k == 7)), reads=[b_wf, b_bcol], writes=[b_pbw], partial=True)
                S.op("dve", lambda e, b=b: e.tensor_scalar(
                    out=fixv[:, :, b], in0=pbw[:, :, b], scalar1=colv[:, 68 + b:69 + b],
                    scalar2=float(SP if b == 0 else SS), op0=ALU.mult, op1=ALU.mult),
                    reads=[b_pbw, b_colv], writes=[b_fixv], partial=True)
            S.barrier()
            S.emit()
        NX = 4
        xts = [sb(es, "a_xt%d" % i, [128, D], F32) for i in range(NX)]
        b_xts = [Buf() for _ in range(NX)]
        junk = sb(es, "a_junk", [128, D], BF16)
        ND = 3
        xns = [sb(es, "a_xn%d" % i, [128, D], BF16) for i in range(ND)]
        b_xns = [Buf() for _ in range(ND)]
        hTs = [sb(es, "a_hT%d" % i, [128, 8, 128], BF16) for i in range(ND)]
        b_hTs = [Buf() for _ in range(ND)]
        us = [sb(es, "a_u%d" % i, [128, 512], BF16) for i in range(ND)]
        b_us = [Buf() for _ in range(ND)]
        ssr = [sb(es, "a_ss%d" % i, [128, 2], F32) for i in range(ND)]
        b_ssr = [Buf() for _ in range(ND)]
        b_rsr = [Buf() for _ in range(ND)]
        HT = sb(es, "a_HT", [128, 4 * 2 * 32 * 128], BF16)
        b_HT = Buf()
        Eps = sb(es, "a_Ep", [128, 64], BF16)
        Ess = sb(es, "a_Es", [128, 256], BF16)
        b_E = Buf()
        S.dma("sp", Eps[:], Ep_d, writes=[b_E], partial=True)
        S.dma("sp", Ess[:], Es_d, writes=[b_E], partial=True)
        Tg = [sb(es, "a_T%d" % i, [128, 2, 256], BF16) for i in range(3)]
        b_Tg = [Buf() for _ in range(3)]
        Hg = [sb(es, "a_Hg%d" % i, [128, 2, 512], BF16) for i in range(2)]
        b_Hg = [Buf(), Buf()]
        Pg = [sb(es, "a_Pg%d" % i, [128, 4, 256], BF16) for i in range(2)]
        b_Pg = [Buf(), Buf()]
        Zf = sb(es, "a_Zf", [128, 4, NOWN], BF16)
        b_Zf = Buf()
        pTs = [ps(es, "a_pT%d" % i, [128, D], BF16) for i in range(2)]
        b_pTs = [Buf(), Buf()]
        pUs = [ps(es, "a_pU%d" % i, [128, 512], F32) for i in range(2)]
        b_pUs = [Buf(), Buf()]
        p1s = [ps(es, "a_p1%d" % i, [128, 1024], F32) for i in range(2)]
        b_p1s = [Buf(), Buf()]

        for seq in range(2):
            if seq == 0:
                row0, Sq, N1, NE, NG = 0, SP, 128, 64, 32
                E_sb = Eps
                HTv = HT[:].rearrange("p (s c) -> p s c", s=128)
            else:
                row0, Sq, N1, NE, NG = SP, SS, 16, 256, 16
                E_sb = Ess
                HTv = HT[:, 0:4 * 2 * 16 * 128].rearrange("p (j r g s k) -> p j r g s k", j=4, r=2, g=16, s=16, k=8)
            xseq = xa[row0:row0 + Sq, :].rearrange("(s2 s1) d -> s1 s2 d", s1=N1)

            def a_s0(t):
                S.dma("sp", xts[t % NX][:], xseq[t], writes=[b_xts[t % NX]])

            def a_s1(t):
                xt, bx = xts[t % NX], b_xts[t % NX]
                ss, rs = ssr[t % ND][:, 0:1], ssr[t % ND][:, 1:2]
                S.op("act", lambda e: e.activation(out=junk[:], in_=xt[:], func=AF.Square, accum_out=ss),
                     reads=[bx], writes=[b_ssr[t % ND]])
                act_rstd(rs, ss, b_rsr[t % ND], b_ssr[t % ND])

            def a_s2(t):
                xt, bx = xts[t % NX], b_xts[t % NX]
                rs = ssr[t % ND][:, 1:2]
                xn = xns[t % ND]
                S.op("pool", lambda e: e.tensor_tensor(out=xn[:], in0=xt[:], in1=rs.to_broadcast([128, D]), op=ALU.mult),
                     reads=[bx, b_rsr[t % ND]], writes=[b_xns[t % ND]])

            def a_s3(t):
                transposes(xns[t % ND], b_xns[t % ND], pTs[t % 2], b_pTs[t % 2])

            def a_s4(t, seq=seq):
                hT, pT = hTs[t % ND], pTs[t % 2]
                S.op("dve", lambda e: e.tensor_copy(out=hT[:].rearrange("p k t -> p (k t)"), in_=pT[:]),
                     reads=[b_pTs[t % 2]], writes=[b_hTs[t % ND]])

            def a_s5(t, seq=seq):
                hT, pU = hTs[t % ND], pUs[t % 2]
                for k in range(8):
                    S.op("pe", lambda e, k=k: e.matmul(pU[:], lhsT=hT[:, k, :], rhs=wfa[seq][:, k, :],
                                                       start=(k == 0), stop=(k == 7)),
                         reads=[b_hTs[t % ND], b_wf], writes=[b_pUs[t % 2]], partial=True)

            def a_s6(t):
                u, pU = us[t % ND], pUs[t % 2]
                S.op("act", lambda e: e.activation(out=u[:], in_=pU[:], func=AF.Copy),
                     reads=[b_pUs[t % 2]], writes=[b_us[t % ND]])

            def a_s7(t, E_sb=E_sb, NE=NE):
                u, p1 = us[t % ND], p1s[t % 2]
                for j in range(4):
                    S.op("pe", lambda e, j=j: e.matmul(p1[:, NE * j:NE * j + NE], lhsT=u[:, 128 * j:128 * j + 128],
                                                       rhs=E_sb[:], start=True, stop=True),
                         reads=[b_us[t % ND], b_E], writes=[b_p1s[t % 2]], partial=True)

            def a_s8(t, seq=seq, NE=NE, HTv=HTv):
                p1 = p1s[t % 2]
                p1v = p1[:, 0:4 * NE].rearrange("p (j e) -> p j e", j=4)
                for r in range(2):
                    if seq == 0:
                        if r == 1:
                            continue
                        dst = HTv[:, t, :]
                        src = p1[:, 0:256]
                    else:
                        dst = HTv[:, :, r, :, t, :]
                        src = p1v.rearrange("p j (g k r) -> p j r g k", r=2, k=8)[:, :, r, :, :]
                    S.op("dve", lambda e, dst=dst, src=src: e.tensor_copy(out=dst, in_=src),
                         reads=[b_p1s[t % 2]], writes=[b_HT], partial=True)

            pipeline([a_s0, a_s1, a_s2, a_s3, a_s4, a_s5, a_s6, a_s7, a_s8], N1)

            scale = 1.0 / float(np.sqrt(Sq * 128.0))
            T_d = Tp_d if seq == 0 else Ts_d

            def g_s0(g, T_d=T_d):
                S.dma("sp", Tg[g % 3][:], T_d[g], writes=[b_Tg[g % 3]])

            def g_s1(g, seq=seq, HTv=HTv):
                pT = pTs[g % 2]
                for r in range(2):
                    for j in range(4):
                        if seq == 0:
                            src = HTv[:, :, j * 64 + r * 32 + g]
                        else:
                            src = HTv[:, j, r, g, :, :].rearrange("p s k -> p (s k)")
                        c0 = 512 * r + 128 * j
                        S.op("pe", lambda e, src=src, c0=c0: e.transpose(out=pT[:, c0:c0 + 128], in_=src,
                                                                         identity=ident[:]),
                             reads=[b_HT, b_ident], writes=[b_pTs[g % 2]], partial=True)

            def g_s2(g):
                pT, H = pTs[g % 2], Hg[g % 2]
                S.op("act", lambda e: e.activation(out=H[:].rearrange("p r c -> p (r c)"), in_=pT[:], func=AF.Copy),
                     reads=[b_pTs[g % 2]], writes=[b_Hg[g % 2]])

            def g_s3(g):
                H, T, p1 = Hg[g % 2], Tg[g % 3], p1s[g % 2]
                for j in range(4):
                    for r in range(2):
                        S.op("pe", lambda e, j=j, r=r: e.matmul(
                            p1[:, 256 * j:256 * j + 256], lhsT=H[:, r, 128 * j:128 * j + 128],
                            rhs=T[:, r, :], start=(r == 0), stop=(r == 1)),
                            reads=[b_Hg[g % 2], b_Tg[g % 3]], writes=[b_p1s[g % 2]], partial=True)

            def g_s4(g, seq=seq):
                P, p1 = Pg[g % 2], p1s[g % 2]
                S.op("dve", lambda e: e.tensor_copy(out=P[:].rearrange("p j c -> p (j c)"), in_=p1[:]),
                     reads=[b_p1s[g % 2]], writes=[b_Pg[g % 2]])
                if g == 0:
                    S.op("dve", lambda e: e.tensor_tensor(
                        out=P[:, :, 0:1], in0=p1[:].rearrange("p (j c) -> p j c", j=4)[:, :, 0:1],
                        in1=fixv[:, :, seq:seq + 1], op=ALU.add),
                        reads=[b_p1s[g % 2], b_fixv, b_Pg[g % 2]], writes=[b_Pg[g % 2]], partial=True)

            def g_s5(g):
                P, pU = Pg[g % 2], pUs[g % 2]
                for j in range(4):
                    for r in range(2):
                        S.op("pe", lambda e, j=j, r=r: e.matmul(
                            pU[:, 128 * j:128 * j + 128], lhsT=CSs[:, r, :], rhs=P[:, j, 128 * r:128 * r + 128],
                            start=(r == 0), stop=(r == 1)),
                            reads=[b_Pg[g % 2], b_CS], writes=[b_pUs[g % 2]], partial=True)

            def g_s6(g, seq=seq, scale=scale):
                pU = pUs[g % 2]
                if seq == 0:
                    dst = Zf[:, :, 0:4096].rearrange("p j (k g) -> p j g k", g=32)[:, :, g, :]
                    src = pU[:].rearrange("p (j k) -> p j k", j=4)
                else:
                    dst = Zf[:, :, 4096:NOWN].rearrange("p j (k b s) -> p j b k s", b=16, s=8)[:, :, g, :, :]
                    src = pU[:].rearrange("p (j k s) -> p j k s", j=4, s=8)
                S.op("dve", lambda e: e.tensor_scalar(out=dst, in0=src, scalar1=scale, scalar2=None, op0=ALU.mult),
                     reads=[b_pUs[g % 2]], writes=[b_Zf], partial=True)

            pipeline([g_s0, g_s1, g_s2, g_s3, g_s4, g_s5, g_s6], NG)
        for j in range(4):
            S.dma("sp", zf_d[j], Zf[:, j, :], reads=[b_Zf], writes=[b_zf], partial=True)
        S.barrier()
        S.emit()

    def load_weight(dst, src, K, cols, stgs, b_stgs, b_dst, ctr):
        step = 4096 // K
        for c0 in range(0, cols, step):
            cw = min(step, cols - c0)
            i = ctr[0] % 2
            ctr[0] += 1
            view = stgs[i][:, 0:K * cw].rearrange("p (k c) -> p k c", k=K)
            S.dma("sp", view, src[:, c0:c0 + cw].rearrange("(k p) c -> p k c", p=128), writes=[b_stgs[i]])
            S.op("dve" if i == 0 else "pool", lambda e, view=view, c0=c0, cw=cw: e.tensor_copy(
                out=dst[:, :, c0:c0 + cw], in_=view), reads=[b_stgs[i]], writes=[b_dst], partial=True)

    def backend(zh, b_zh, junk, ssm, b_ssm, wk, b_wk, gg, tmp, b_tmp, xres, b_xres, dst_rows, b_dst):
        for n in range(2):
            S.op("act", lambda e, n=n: e.activation(out=junk[:, 0:512], in_=zh[n], func=AF.Square,
                                                    accum_out=ssm[:, n:n + 1]),
                 reads=[b_zh[n]], writes=[b_ssm], partial=(n == 1))
        dve_rstd(wk, b_wk, [ssm[:, 0:1], ssm[:, 1:2]], b_ssm, ssm[:, 3:4], b_ssm)
        for n in range(2):
            S.op("dve", lambda e, n=n: e.scalar_tensor_tensor(
                out=tmp[:, 512 * n:512 * n + 512], in0=zh[n], scalar=ssm[:, 3:4],
                in1=gg[:, 512 * n:512 * n + 512], op0=ALU.mult, op1=ALU.mult),
                reads=[b_zh[n], b_ssm, b_rowg], writes=[b_tmp], partial=(n == 1))
        S.op("pool", lambda e: e.tensor_tensor(out=tmp[:], in0=tmp[:], in1=xres[:], op=ALU.add),
             reads=[b_tmp, b_xres], writes=[b_tmp])
        return S.dma("sp", dst_rows, tmp[:], reads=[b_tmp], writes=[b_dst], partial=True)

    with ExitStack() as es:
        wi = sb(es, "b1_wi", [128, 8, 2560], BF16)
        wfo = sb(es, "b1_wfo", [128, 4, D], BF16)
        wpo = sb(es, "b1_wpo", [128, 4, D], BF16)
        wou = sb(es, "b1_wou", [128, 8, D], BF16)
        wpg = sb(es, "b1_wpg", [128, 4, 128], BF16)
        b_w = Buf()
        for c0 in range(0, 2560, 640):
            S.dma("sp", wi[:, :, c0:c0 + 640], win_bf[:, c0:c0 + 640].rearrange("(k p) c -> p k c", p=128),
                  reads=[b_wscr], writes=[b_w], partial=True)
        S.dma("sp", wfo[:], wfo_bf.rearrange("(k p) c -> p k c", p=128), reads=[b_wscr], writes=[b_w], partial=True)
        S.dma("sp", wpo[:], wpo_bf.rearrange("(k p) c -> p k c", p=128), reads=[b_wscr], writes=[b_w], partial=True)
        for c0 in range(0, D, 512):
            S.dma("sp", wou[:, :, c0:c0 + 512], wout_bf[:, c0:c0 + 512].rearrange("(k p) c -> p k c", p=128),
                  reads=[b_wscr], writes=[b_w], partial=True)
        S.dma("sp", wpg[:], wpg_bf.rearrange("g c d -> c g d"), reads=[b_wscr], writes=[b_w], partial=True)
        S.barrier()
        S.emit()
        NX = 2
        xts = [sb(es, "b1_xt%d" % i, [128, D], F32) for i in range(NX)]
        b_xts = [Buf() for _ in range(NX)]
        junk = sb(es, "b1_junk", [128, D], BF16)
        NXN = 3
        xns = [sb(es, "b1_xn%d" % i, [128, D], BF16) for i in range(NXN)]
        b_xns = [Buf() for _ in range(NXN)]
        ssr = [sb(es, "b1_ss%d" % i, [128, 8], F32) for i in range(NXN)]
        b_ssr = [Buf() for _ in range(NXN)]
        b_rsr = [Buf() for _ in range(NXN)]
        b_wkf = [Buf() for _ in range(NXN)]
        hTs = [sb(es, "b1_hT%d" % i, [128, 8, 768], BF16) for i in range(2)]
        b_hTs = [Buf(), Buf()]
        up = sb(es, "b1_up", [128, 4, 768], F32)
        b_up = Buf()
        A2 = sb(es, "b1_A2", [128, 16, 48], F32)
        A4 = sb(es, "b1_A4", [128, 16, 48], F32)
        A8 = sb(es, "b1_A8", [128, 16, 48], F32)
        win = sb(es, "b1_win", [128, 16, 32], F32)
        b_A2, b_A4, b_A8, b_win = Buf(), Buf(), Buf(), Buf()
        dbf = sb(es, "b1_d", [128, 4, 512], BF16)
        b_d = Buf()
        pm1 = sb(es, "b1_pm", [128, 4, 512], BF16)
        pms = [pm1, pm1]
        b_pm1 = Buf()
        b_pms = [b_pm1, b_pm1]
        mskt = sb(es, "b1_msk", [128, 768], F32)
        b_msk = Buf()
        icnt = sb(es, "b1_icn", [128, 4, 512], F32)
        b_icn = Buf()
        zft = sb(es, "b1_zf", [128, 4, 512], BF16)
        b_zft = Buf()
        gfs = sb(es, "b1_gf", [128, 512], F32)
        gps = sb(es, "b1_gp", [128, 512], F32)
        t1 = sb(es, "b1_t1", [128, 512], F32)
        t2 = sb(es, "b1_t2", [128, 512], F32)
        b_gf, b_gp, b_t1, b_t2 = Buf(), Buf(), Buf(), Buf()
        mg = sb(es, "b1_mg", [128, 8, 512], BF16)
        b_mg = Buf()
        xres = sb(es, "b1_xres", [128, D], F32)
        tmp = sb(es, "b1_tmp", [128, D], F32)
        b_xres, b_tmp = Buf(), Buf()
        ssm = sb(es, "b1_ssm", [128, 8], F32)
        b_ssm, b_wkb = Buf(), Buf()
        pT = ps(es, "b1_pT", [128, D], BF16)
        b_pT = Buf()
        pUM = ps(es, "b1_pUM", [128, 1024], F32)
        b_pUMh = [Buf(), Buf()]
        pG = [ps(es, "b1_pG%d" % i, [128, 512], F32) for i in range(4)]
        b_pG = [Buf() for _ in range(4)]

        fe_ctr = [0]
        def b1_fea(mt, i):
            tcn = fe_ctr[0]
            fe_ctr[0] += 1
            ix, i3 = tcn % NX, tcn % NXN
            r0 = 768 * mt + 128 * i
            xt, bx = xts[ix], b_xts[ix]
            S.dma("sp", xt[:], xb[r0:r0 + 128, :], writes=[bx])
            sst = ssr[i3]
            S.op("act", lambda e: e.activation(out=junk[:], in_=xt[:], func=AF.Square, accum_out=sst[:, 0:1]),
                 reads=[bx], writes=[b_ssr[i3]])
            dve_rstd(sst[:, 4:8], b_wkf[i3], [sst[:, 0:1]], b_ssr[i3], sst[:, 1:2], b_rsr[i3])
            xn = xns[i3]
            S.op("pool", lambda e: e.tensor_tensor(out=xn[:], in0=xt[:], in1=sst[:, 1:2].to_broadcast([128, D]),
                                                   op=ALU.mult), reads=[bx, b_rsr[i3]], writes=[b_xns[i3]])
            return i3

        def b1_feb(mt, i, i3):
            hT = hTs[mt % 2]
            bi = 0 if mt < 8 else 1
            transposes(xns[i3], b_xns[i3], pT, b_pT)
            evacs(pT, b_pT, (lambda k: hT[:, k, 128 * i:128 * i + 128]), b_hTs[mt % 2], bi, 0)

        def b1_up(mt, g, alt=False):
            hT = hTs[mt % 2]
            if alt:
                banks = [(pG[0][:, 0:384], b_pG[0]), (pG[1][:, 0:384], b_pG[1])]
            else:
                banks = [(pUM[:, 0:384], b_pUMh[0]), (pUM[:, 512:896], b_pUMh[1])]
            if g == 0:
                S.dma("sp", mskt[:], msk_d[:, 768 * mt:768 * mt + 768], writes=[b_msk])
                S.dma("sp", icnt[:], icn_d[:, :, 512 * mt:512 * mt + 512], writes=[b_icn])
            for h in range(2):
                for k in range(8):
                    S.op("pe", lambda e, h=h, k=k: e.matmul(
                        banks[h][0], lhsT=wi[:, k, 128 * g:128 * g + 128],
                        rhs=hT[:, k, 384 * h:384 * h + 384], start=(k == 0), stop=(k == 7)),
                        reads=[b_w, b_hTs[mt % 2]], writes=[banks[h][1]], partial=True)
            for h in range(2):
                S.op("dve", lambda e, h=h: e.tensor_tensor(
                    out=up[:, g, 384 * h:384 * h + 384], in0=banks[h][0],
                    in1=mskt[:, 384 * h:384 * h + 384], op=ALU.mult),
                    reads=[banks[h][1], b_msk], writes=[b_up], partial=True)

        def b1_pool(mt, g):
            U = up[:, g, :].rearrange("p (r c) -> p r c", c=48)

            def tt(out, a, b, rd, wr, op=ALU.add):
                S.op("pool", lambda e: e.tensor_tensor(out=out, in0=a, in1=b, op=op), reads=rd, writes=wr)
            if g == 0:
                tt(win[:], U[:, :, 7:39], U[:, :, 8:40], [b_up], [b_win])
            else:
                tt(A2[:, :, 0:47], U[:, :, 0:47], U[:, :, 1:48], [b_up], [b_A2])
                if g == 1:
                    tt(win[:], A2[:, :, 6:38], A2[:, :, 8:40], [b_A2], [b_win])
                else:
                    tt(A4[:, :, 0:45], A2[:, :, 0:45], A2[:, :, 2:47], [b_A2], [b_A4])
                    if g == 2:
                        tt(win[:], A4[:, :, 4:36], A4[:, :, 8:40], [b_A4], [b_win])
                    else:
                        tt(A8[:, :, 0:41], A4[:, :, 0:41], A4[:, :, 4:45], [b_A4], [b_A8])
                        tt(win[:], A8[:, :, 0:32], A8[:, :, 8:40], [b_A8], [b_win])
            tt(win[:], win[:], icnt[:, g, :].rearrange("p (r c) -> p r c", c=32), [b_win, b_icn], [b_win], op=ALU.mult)
            tt(dbf[:, g, :].rearrange("p (r c) -> p r c", c=32), win[:], U[:, :, 8:40], [b_win, b_up], [b_d],
               op=ALU.subtract)

        def b1_pg(mt):
            pm = pms[mt % 2]
            for g in range(4):
                ob, bb = pUM[:, 512 * (g % 2):512 * (g % 2) + 512], b_pUMh[g % 2]
                S.op("pe", lambda e, g=g, ob=ob: e.matmul(ob, lhsT=wpg[:, g, :], rhs=dbf[:, g, :], start=True, stop=True),
                     reads=[b_w, b_d], writes=[bb], partial=True)
                S.op("dve", lambda e, g=g, ob=ob: e.tensor_scalar(out=pm[:, g, :], in0=ob, scalar1=colv[:, 64 + g:65 + g],
                                                                  scalar2=None, op0=ALU.mult),
                     reads=[bb, b_colv], writes=[b_pms[mt % 2]], partial=True)

        def b1_gc(mt, c):
            hT, pm = hTs[mt % 2], pms[mt % 2]
            if c == 0:
                S.dma("sp", zft[:], zf_d[:, :, 512 * mt:512 * mt + 512].rearrange("j p t -> p j t"),
                      reads=[b_zf], writes=[b_zft])
            for pb, col0 in ((0, 512 + 128 * c), (1, 1536 + 128 * c)):
                for k in range(8):
                    S.op("pe", lambda e, pb=pb, col0=col0, k=k: e.matmul(
                        pG[pb][:].rearrange("p (r c) -> p r c", c=32), lhsT=wi[:, k, col0:col0 + 128],
                        rhs=hT[:, k, :].rearrange("p (r c) -> p r c", c=48)[:, :, 8:40],
                        start=(k == 0), stop=(k == 7)),
                        reads=[b_w, b_hTs[mt % 2]], writes=[b_pG[pb]], partial=True)
                gdst, b_g = (gfs, b_gf) if pb == 0 else (gps, b_gp)
                S.op("act", lambda e, gdst=gdst, pb=pb: e.activation(out=gdst[:], in_=pG[pb][:], func=AF.Sigmoid),
                     reads=[b_pG[pb]], writes=[b_g])
            for k in range(4):
                S.op("pe", lambda e, k=k: e.matmul(pG[3][:], lhsT=wpo[:, k, 128 * c:128 * c + 128],
                                                   rhs=pm[:, k, :], start=(k == 0), stop=(k == 3)),
                     reads=[b_w, b_pms[mt % 2]], writes=[b_pG[3]], partial=True)
            for k in range(4):
                S.op("pe", lambda e, k=k: e.matmul(pG[2][:], lhsT=wfo[:, k, 128 * c:128 * c + 128],
                                                   rhs=zft[:, k, :], start=(k == 0), stop=(k == 3)),
                     reads=[b_w, b_zft], writes=[b_pG[2]], partial=True)
            S.op("dve", lambda e: e.tensor_tensor(out=t1[:], in0=pG[2][:], in1=gfs[:], op=ALU.mult),
                 reads=[b_pG[2], b_gf], writes=[b_t1])
            S.op("dve", lambda e: e.tensor_tensor(out=t2[:], in0=pG[3][:], in1=gps[:], op=ALU.mult),
                 reads=[b_pG[3], b_gp], writes=[b_t2])
            S.op("pool", lambda e: e.tensor_tensor(out=mg[:, c, :], in0=t1[:], in1=t2[:], op=ALU.add),
                 reads=[b_t1, b_t2], writes=[b_mg], partial=True)

        def b1_wo(mt, s):
            bi = 0 if mt < 8 else 1
            if s % 2 == 0:
                zh, bz = [pUM[:, 0:512], pUM[:, 512:1024]], b_pUMh
            else:
                zh, bz = [pG[2][:], pG[3][:]], [b_pG[2], b_pG[3]]
            for n in range(2):
                for c in range(8):
                    S.op("pe", lambda e, n=n, c=c: e.matmul(
                        zh[n], lhsT=mg[:, c, 128 * s:128 * s + 128], rhs=wou[:, c, 512 * n:512 * n + 512],
                        start=(c == 0), stop=(c == 7)),
                        reads=[b_w, b_mg], writes=[bz[n]], partial=True)
            for rr in range(4):
                run = 16 * mt + 4 * s + rr
                S.dma("sp", xres[32 * rr:32 * rr + 32, :], xb[48 * run + 8:48 * run + 40, :],
                      writes=[b_xres], partial=True)
            r0 = 512 * mt + 128 * s
            backend(zh, bz, junk, ssm, b_ssm, ssm[:, 4:8], b_wkb, rowg1[:, bi, :], tmp, b_tmp, xres, b_xres,
                    x1_d[r0:r0 + 128, :], b_x1)

        def b1_fe_sched(mt):
            slots = [[] for _ in range(8)]
            ids = {}

            def mk_a(i):
                return lambda: ids.__setitem__(i, b1_fea(mt, i))

            def mk_b(i):
                return lambda: b1_feb(mt, i, ids[i])
            plan = {0: [("a", 0), ("a", 1)], 1: [("b", 0), ("a", 2)], 2: [("b", 1), ("a", 3)], 3: [("b", 2), ("a", 4)],
                    4: [("b", 3), ("a", 5)], 5: [("b", 4)], 6: [("b", 5)]}
            for c, lst in plan.items():
                for kind, i in lst:
                    slots[c].append(mk_a(i) if kind == "a" else mk_b(i))
            return slots

        for slot in b1_fe_sched(0):
            for f in slot:
                f()
        for g in range(4):
            b1_up(0, g)
            b1_pool(0, g)
        for mt in range(NMT):
            b1_pg(mt)
            nxt = b1_fe_sched(mt + 1) if mt + 1 < NMT else [[] for _ in range(8)]
            for c in range(8):
                b1_gc(mt, c)
                for f in nxt[c]:
                    f()
                if mt + 1 < NMT and c >= 6:
                    b1_up(mt + 1, c - 6)
                    b1_pool(mt + 1, c - 6)
            for s in range(4):
                b1_wo(mt, s)
                if mt + 1 < NMT and s < 2:
                    b1_up(mt + 1, 2 + s, alt=True)
                    b1_pool(mt + 1, 2 + s)
        S.barrier()
        S.emit()
    mid.close()

    with ExitStack() as es:
        wg = sb(es, "b2_wg", [128, 8, DFF], BF16)
        wu = sb(es, "b2_wu", [128, 8, DFF], BF16)
        wd = sb(es, "b2_wd", [128, NJ, D], BF16)
        b_w = Buf()
        for c0 in range(0, DFF, 704):
            S.dma("sp", wg[:, :, c0:c0 + 704], wg_bf[:, c0:c0 + 704].rearrange("(k p) c -> p k c", p=128),
                  reads=[b_wscr], writes=[b_w], partial=True)
            S.dma("sp", wu[:, :, c0:c0 + 704], wu_bf[:, c0:c0 + 704].rearrange("(k p) c -> p k c", p=128),
                  reads=[b_wscr], writes=[b_w], partial=True)
        for j0 in range(0, NJ, 6):
            nj = min(6, NJ - j0)
            S.dma("sp", wd[:, j0:j0 + nj, :], wd_bf[128 * j0:128 * (j0 + nj), :].rearrange("(k p) c -> p k c", p=128),
                  reads=[b_wscr], writes=[b_w], partial=True)
        S.barrier()
        S.emit()
        NX = 2
        xts = [sb(es, "b2_xt%d" % i, [128, D], F32) for i in range(NX)]
        b_xts = [Buf() for _ in range(NX)]
        junk = sb(es, "b2_junk", [128, D], BF16)
        xns = [sb(es, "b2_xn%d" % i, [128, D], BF16) for i in range(2)]
        b_xns = [Buf(), Buf()]
        ssr = [sb(es, "b2_ss%d" % i, [128, 8], F32) for i in range(2)]
        b_ssr = [Buf(), Buf()]
        b_rsr = [Buf(), Buf()]
        b_wkf = [Buf(), Buf()]
        hTs = [sb(es, "b2_hT%d" % i, [128, 8, 512], BF16) for i in range(2)]
        b_hTs = [Buf(), Buf()]
        sa = [sb(es, "b2_sa%d" % i, [128, 512], F32) for i in range(2)]
        b_sa = [Buf(), Buf()]
        aT = sb(es, "b2_aT", [128, NJ, 512], BF16)
        b_aT = Buf()
        xres = sb(es, "b2_xres", [128, D], F32)
        tmp = sb(es, "b2_tmp", [128, D], F32)
        b_xres, b_tmp = Buf(), Buf()
        ssm = sb(es, "b2_ssm", [128, 8], F32)
        b_ssm, b_wkb = Buf(), Buf()
        b_y = Buf()
        pT = ps(es, "b2_pT", [128, D], BF16)
        b_pT = Buf()
        pA = [ps(es, "b2_pA%d" % i, [128, 512], F32) for i in range(2)]
        pB = [ps(es, "b2_pB%d" % i, [128, 512], F32) for i in range(2)]
        b_pA = [Buf(), Buf()]
        b_pB = [Buf(), Buf()]
        pZ = ps(es, "b2_pZ", [128, 1024], F32)
        b_pZh = [Buf(), Buf()]
        fe_ctr = [0]

        def b2_fea(mt, s):
            tcn = fe_ctr[0]
            fe_ctr[0] += 1
            ix, i2 = tcn % NX, tcn % 2
            r0 = 512 * mt + 128 * s
            xt, bx = xts[ix], b_xts[ix]
            S.dma("sp", xt[:], x1_d[r0:r0 + 128, :], reads=[b_x1], writes=[bx])
            sst = ssr[i2]
            S.op("act", lambda e: e.activation(out=junk[:], in_=xt[:], func=AF.Square, accum_out=sst[:, 0:1]),
                 reads=[bx], writes=[b_ssr[i2]])
            dve_rstd(sst[:, 4:8], b_wkf[i2], [sst[:, 0:1]], b_ssr[i2], sst[:, 1:2], b_rsr[i2])
            xn = xns[i2]
            S.op("pool", lambda e: e.tensor_tensor(out=xn[:], in0=xt[:], in1=sst[:, 1:2].to_broadcast([128, D]),
                                                   op=ALU.mult), reads=[bx, b_rsr[i2]], writes=[b_xns[i2]])
            return i2

        def b2_feb(mt, s, i2):
            hT = hTs[mt % 2]
            bi = 0 if mt < 8 else 1
            transposes(xns[i2], b_xns[i2], pT, b_pT)
            evacs(pT, b_pT, (lambda k: hT[:, k, 128 * s:128 * s + 128]), b_hTs[mt % 2], bi, 1)

        def b2_fe_sched(mt):
            slots = {}
            ids = {}

            def mk_a(i):
                return lambda: ids.__setitem__(i, b2_fea(mt, i))

            def mk_b(i):
                return lambda: b2_feb(mt, i, ids[i])
            plan = {1: [("a", 0)], 4: [("b", 0), ("a", 1)], 8: [("b", 1), ("a", 2)], 12: [("b", 2), ("a", 3)],
                    16: [("b", 3)]}
            for j, lst in plan.items():
                slots[j] = [mk_a(i) if kind == "a" else mk_b(i) for kind, i in lst]
            return slots

        def b2_gu(mt, j):
            hT = hTs[mt % 2]
            i2 = j % 2
            for k in range(8):
                S.op("pe", lambda e, k=k: e.matmul(pA[i2][:], lhsT=wg[:, k, 128 * j:128 * j + 128],
                                                   rhs=hT[:, k, :], start=(k == 0), stop=(k == 7)),
                     reads=[b_w, b_hTs[mt % 2]], writes=[b_pA[i2]], partial=True)
            for k in range(8):
                S.op("pe", lambda e, k=k: e.matmul(pB[i2][:], lhsT=wu[:, k, 128 * j:128 * j + 128],
                                                   rhs=hT[:, k, :], start=(k == 0), stop=(k == 7)),
                     reads=[b_w, b_hTs[mt % 2]], writes=[b_pB[i2]], partial=True)
            S.op("act", lambda e: e.activation(out=sa[i2][:], in_=pA[i2][:], func=AF.Silu),
                 reads=[b_pA[i2]], writes=[b_sa[i2]])
            S.op("dve", lambda e: e.tensor_tensor(out=aT[:, j, :], in0=pB[i2][:], in1=sa[i2][:], op=ALU.mult),
                 reads=[b_pB[i2], b_sa[i2]], writes=[b_aT], partial=True)

        def b2_dn(mt, s):
            bi = 0 if mt < 8 else 1
            if s % 2 == 0:
                zh, bz = [pZ[:, 0:512], pZ[:, 512:1024]], b_pZh
            else:
                zh, bz = [pA[0][:], pB[0][:]], [b_pA[0], b_pB[0]]
            for n in range(2):
                for j in range(NJ):
                    S.op("pe", lambda e, n=n, j=j: e.matmul(
                        zh[n], lhsT=aT[:, j, 128 * s:128 * s + 128], rhs=wd[:, j, 512 * n:512 * n + 512],
                        start=(j == 0), stop=(j == NJ - 1)),
                        reads=[b_w, b_aT], writes=[bz[n]], partial=True)
            r0 = 512 * mt + 128 * s
            S.dma("sp", xres[:], x1_d[r0:r0 + 128, :], reads=[b_x1], writes=[b_xres])
            out_toks.append(backend(zh, bz, junk, ssm, b_ssm, ssm[:, 4:8], b_wkb, rowg2[:, bi, :], tmp, b_tmp,
                                    xres, b_xres, y_d[r0:r0 + 128, :], b_y))

        sl = b2_fe_sched(0)
        for j in sorted(sl):
            for f in sl[j]:
                f()
        for mt in range(NMT):
            nxt = b2_fe_sched(mt + 1) if mt + 1 < NMT else {}
            for j in range(NJ):
                b2_gu(mt, j)
                for f in nxt.get(j, []):
                    f()
            for s in range(4):
                b2_dn(mt, s)
        S.barrier()
        S.emit()
    top.close()
    return nc


_CACHE = {}


def _core_inputs(core, x_prompt, x_sample, c_prompt, c_sample, w, shared):
    b, q = core // 4, core % 4
    xa = np.concatenate([x_prompt[b], x_sample[core]], axis=0)
    xb = np.zeros((NRUN, 48, D), np.float32)
    msk = np.zeros((NRUN, 48), np.float32)
    for r, (sid, start, S) in enumerate(run_starts(q)):
        src = x_prompt[b] if sid == 0 else x_sample[core]
        lo, hi = start - 8, start + 40
        l2, h2 = max(lo, 0), min(hi, S)
        xb[r, l2 - lo:h2 - lo] = src[l2:h2]
        msk[r, l2 - lo:h2 - lo] = 1.0
    colv = np.zeros((128, 70), np.float32)

    def col(v):
        return np.ascontiguousarray(v.reshape(-1, 128).T)
    colv[:, 0:8] = col(c_prompt[b])
    colv[:, 8:16] = col(c_sample[core])
    colv[:, 16:24] = col(w["g_pre_mix"])
    colv[:, 24:32] = col(w["g_pre_ffn"])
    ba = w["b_ada"]
    colv[:, 32:40] = col(ba[0:D])
    colv[:, 40:48] = col(ba[D:2 * D])
    colv[:, 48:56] = col(ba[3 * D:4 * D])
    colv[:, 56:64] = col(ba[4 * D:5 * D])
    colv[:, 64:68] = col(w["pool_scale"])
    colv[:, 68] = 1.0 if q == 0 else 0.0
    colv[:, 69] = 1.0
    rowv = np.stack([w["g_post_mix"], w["g_post_ffn"], ba[2 * D:3 * D], ba[5 * D:6 * D]], axis=0)
    rowv = np.ascontiguousarray(np.broadcast_to(rowv[None], (128, 4, D)))
    E_p, T_p, E_s, T_s, CS = shared["four"][q]
    m = {
        "xa": np.ascontiguousarray(xa), "xb": xb.reshape(NRUN * 48, D),
        "msk": np.ascontiguousarray(np.broadcast_to(msk.reshape(1, -1), (128, NRUN * 48))),
        "icn": np.ascontiguousarray(np.broadcast_to(shared["icn"][q][None], (128, 4, NOWN))),
        "colv": colv, "rowv": rowv, "ident": shared["ident"],
        "Ep": E_p, "Tp": T_p, "Es": E_s, "Ts": T_s, "CS": CS,
        "w_ada": w["w_ada"], "w_in": w["w_in"], "w_fo": w["w_fo"], "w_pg": w["w_pg"], "w_po": w["w_po"],
        "w_out": w["w_out"], "w_gate": w["w_gate"], "w_up": w["w_up"], "w_down": w["w_down"],
    }
    return m


def kernel(x_prompt, x_sample, c_prompt, c_sample, w_ada, b_ada, g_pre_mix, w_in, w_fo, w_pg, pool_scale, w_po,
           w_out, g_post_mix, g_pre_ffn, w_gate, w_up, w_down, g_post_ffn):
    f = lambda a: np.ascontiguousarray(np.asarray(a, dtype=np.float32))
    x_prompt, x_sample, c_prompt, c_sample = f(x_prompt), f(x_sample), f(c_prompt), f(c_sample)
    w = {"w_ada": f(w_ada)[0], "b_ada": f(b_ada)[0], "g_pre_mix": f(g_pre_mix)[0], "w_in": f(w_in)[0],
         "w_fo": f(w_fo)[0], "w_pg": f(w_pg)[0], "pool_scale": f(pool_scale)[0], "w_po": f(w_po)[0],
         "w_out": f(w_out)[0], "g_post_mix": f(g_post_mix)[0], "g_pre_ffn": f(g_pre_ffn)[0],
         "w_gate": f(w_gate)[0], "w_up": f(w_up)[0], "w_down": f(w_down)[0], "g_post_ffn": f(g_post_ffn)[0]}
    if "shared" not in _CACHE:
        _CACHE["shared"] = {
            "four": [fourier_consts(q) for q in range(4)],
            "icn": [inv_counts(q) for q in range(4)],
            "ident": np.eye(128, dtype=np.float32).astype(BF),
        }
    shared = _CACHE["shared"]
    if "nc" not in _CACHE:
        _CACHE["nc"] = build_program()
    nc = _CACHE["nc"]
    in_maps = [_core_inputs(c, x_prompt, x_sample, c_prompt, c_sample, w, shared) for c in range(8)]
    res = run_bass_kernel_spmd(nc, in_maps, core_ids=list(range(8)))
    y_prompt = np.empty((2, SP, D), np.float32)
    y_sample = np.empty((8, SS, D), np.float32)
    for c in range(8):
        b, q = c // 4, c % 4
        y = np.asarray(res.results[c]["y"], dtype=np.float32)
        yp = y[:4096].reshape(128, 32, D)
        y_prompt[b].reshape(128, 128, D)[:, 32 * q:32 * q + 32, :] = yp
        y_sample[c] = y[4096:]
    return (y_prompt, y_sample)
```

```python
from contextlib import ExitStack
import numpy as np
import ml_dtypes
import concourse.bass as bass
import concourse.mybir as mybir
from concourse.bass_utils import run_bass_kernel_spmd

F32 = mybir.dt.float32
BF16 = mybir.dt.bfloat16
ALU = mybir.AluOpType
AF = mybir.ActivationFunctionType
BF = ml_dtypes.bfloat16

D = 1024
SP = 16384
SS = 2048
DFF = 2816
NJ = 22
NRUN = 192
NOWN = 6144
NMT = 12
EPS = 1e-6
WINS = (2, 4, 8, 16)


class Buf:
    __slots__ = ("w", "r", "prev")

    def __init__(self):
        self.w = []
        self.r = []
        self.prev = []


class Sched:
    ENG = ("pe", "act", "dve", "pool", "sp")

    def __init__(self, nc):
        self.nc = nc
        self.q = {e: [] for e in self.ENG}
        self.cnt = {e: 0 for e in self.ENG}
        self.sem = {e: nc.alloc_semaphore(name="ms_" + e) for e in self.ENG}
        self.seen = {e: {} for e in self.ENG}
        self.dma_sems = [nc.alloc_semaphore(name="dq%d" % i) for i in range(32)]
        self.dma_cnt = [0] * len(self.dma_sems)
        self.dma_rr = 0

    @staticmethod
    def _merge(toks):
        d = {}
        for k, v in toks:
            if d.get(k, 0) < v:
                d[k] = v
        return list(d.items())

    def _deps(self, reads, writes, partial):
        deps = []
        for b in reads:
            deps += b.w
        for b in writes:
            if partial and not b.r:
                deps += b.prev
            else:
                b.prev = self._merge(b.w + b.r)
                b.w = []
                b.r = []
                deps += b.prev
        return self._merge(deps)

    def _filter(self, eng, deps):
        out = []
        seen = self.seen[eng]
        for k, v in deps:
            if eng == "pe" and k == "pe":
                continue
            if seen.get(k, 0) >= v:
                continue
            seen[k] = v
            out.append((k, v))
        return out

    def op(self, eng, fn, reads=(), writes=(), partial=False):
        deps = self._filter(eng, self._deps(reads, writes, partial))
        self.cnt[eng] += 1
        tok = (eng, self.cnt[eng])
        for b in reads:
            b.r = self._merge(b.r + [tok])
        for b in writes:
            b.w = self._merge(b.w + [tok])
        self.q[eng].append((fn, deps, None))
        return tok

    def dma(self, eng, out, in_, reads=(), writes=(), partial=False):
        i = self.dma_rr
        self.dma_rr = (self.dma_rr + 1) % len(self.dma_sems)
        deps = self._deps(reads, writes, partial)
        if self.dma_cnt[i] > 0:
            deps = self._merge(deps + [(("dma", i), self.dma_cnt[i])])
        deps = self._filter(eng, deps)
        self.dma_cnt[i] += 16
        tok = (("dma", i), self.dma_cnt[i])
        for b in reads:
            b.r = self._merge(b.r + [tok])
        for b in writes:
            b.w = self._merge(b.w + [tok])
        self.q[eng].append((lambda e: e.dma_start(out=out, in_=in_), deps, i))
        return tok

    def barrier(self):
        toks = [(e, self.cnt[e]) for e in self.ENG if self.cnt[e] > 0]
        toks += [(("dma", i), c) for i, c in enumerate(self.dma_cnt) if c > 0]
        for e in self.ENG:
            deps = self._filter(e, list(toks))
            self.q[e].append((None, deps, None))

    def _semof(self, k):
        if isinstance(k, tuple):
            return self.dma_sems[k[1]]
        return self.sem[k]

    def emit(self):
        def run(eng_name):
            def body(e):
                for fn, deps, dsem in self.q[eng_name]:
                    for k, v in deps:
                        e.wait_ge(self._semof(k), v)
                    if fn is None:
                        continue
                    ins = fn(e)
                    if dsem is None:
                        ins.then_inc(self.sem[eng_name], 1)
                    else:
                        ins.then_inc(self.dma_sems[dsem], 16)
            return body

        with self.nc.Block() as block:
            block.tensor(run("pe"))
            block.scalar(run("act"))
            block.vector(run("dve"))
            block.gpsimd(run("pool"))
            block.sync(run("sp"))
        self.q = {e: [] for e in self.ENG}


def fourier_consts(q):
    s2 = np.arange(128)
    k2 = 32 * q + np.arange(32)
    ang = 2 * np.pi * ((np.outer(s2, k2)) % 128) / 128.0
    E_p = np.empty((128, 2, 32))
    E_p[:, 0, :] = np.cos(ang)
    E_p[:, 1, :] = -np.sin(ang)
    E_p = E_p.reshape(128, 64)
    s1 = np.arange(128)
    k1 = np.arange(128)
    T_p = np.zeros((32, 128, 2, 256))
    for g in range(32):
        k = 128 * k1 + k2[g]
        th = 2 * np.pi * ((np.outer(s1, k)) % SP) / float(SP)
        Tr, Ti = np.cos(th), -np.sin(th)
        T_p[g, :, 0, :128] = Tr
        T_p[g, :, 0, 128:] = Ti
        T_p[g, :, 1, :128] = -Ti
        T_p[g, :, 1, 128:] = Tr
    ang = 2 * np.pi * ((np.outer(s2, np.arange(128))) % 128) / 128.0
    E_s = np.empty((128, 128, 2))
    E_s[..., 0] = np.cos(ang)
    E_s[..., 1] = -np.sin(ang)
    E_s = E_s.reshape(128, 256)
    T_s = np.zeros((16, 128, 2, 256))
    for B in range(16):
        for s1v in range(16):
            for ks in range(8):
                r = s1v * 8 + ks
                k1v = np.arange(16)
                k = 128 * k1v + 8 * B + ks
                th = 2 * np.pi * ((k * s1v) % SS) / float(SS)
                Tr, Ti = np.cos(th), -np.sin(th)
                col = k1v * 8 + ks
                T_s[B, r, 0, col] = Tr
                T_s[B, r, 0, 128 + col] = Ti
                T_s[B, r, 1, col] = -Ti
                T_s[B, r, 1, 128 + col] = Tr
    c = np.arange(128)
    angc = 2 * np.pi * ((np.outer(c, c)) % 128) / 128.0
    CS = np.stack([np.cos(angc), np.sin(angc)], axis=1)
    return (E_p.astype(BF), T_p.astype(BF), E_s.astype(BF), T_s.astype(BF), CS.astype(BF))


def run_starts(q):
    st = [(0, 128 * k1 + 32 * q, SP) for k1 in range(128)]
    st += [(1, 32 * r, SS) for r in range(64)]
    return st


def inv_counts(q):
    icn = np.zeros((4, NOWN), np.float32)
    for r, (_, start, S) in enumerate(run_starts(q)):
        t = start + np.arange(32)
        for g, w in enumerate(WINS):
            left = w // 2
            right = w - 1 - left
            cnt = np.minimum(t + right, S - 1) - np.maximum(t - left, 0) + 1
            icn[g, 32 * r:32 * r + 32] = 1.0 / cnt.astype(np.float32)
    return icn


def build_program():
    nc = bass.Bass("TRN2", target_bir_lowering=False)

    def din(name, shape, dt=F32):
        return nc.dram_tensor(name, list(shape), dt, kind="ExternalInput").ap()

    xa = din("xa", [SP + SS, D])
    xb = din("xb", [NRUN * 48, D])
    msk_d = din("msk", [128, NRUN * 48])
    icn_d = din("icn", [128, 4, NOWN])
    colv_d = din("colv", [128, 70])
    rowv_d = din("rowv", [128, 4, D])
    ident_d = din("ident", [128, 128], BF16)
    Ep_d = din("Ep", [128, 64], BF16)
    Tp_d = din("Tp", [32, 128, 2, 256], BF16)
    Es_d = din("Es", [128, 256], BF16)
    Ts_d = din("Ts", [16, 128, 2, 256], BF16)
    CS_d = din("CS", [128, 2, 128], BF16)
    w_ada = din("w_ada", [D, 6 * D])
    w_in = din("w_in", [D, 3072])
    w_fo = din("w_fo", [512, D])
    w_pg = din("w_pg", [4, 128, 128])
    w_po = din("w_po", [512, D])
    w_out = din("w_out", [D, D])
    w_gate = din("w_gate", [D, DFF])
    w_up = din("w_up", [D, DFF])
    w_down = din("w_down", [DFF, D])
    y_d = nc.dram_tensor("y", [NOWN, D], F32, kind="ExternalOutput").ap()
    zf_d = nc.dram_tensor("zf_scr", [4, 128, NOWN], BF16).ap()
    x1_d = nc.dram_tensor("x1_scr", [NOWN, D], F32).ap()
    wg_bf = nc.dram_tensor("wg_bf", [D, DFF], BF16).ap()
    wu_bf = nc.dram_tensor("wu_bf", [D, DFF], BF16).ap()
    wd_bf = nc.dram_tensor("wd_bf", [DFF, D], BF16).ap()
    win_bf = nc.dram_tensor("win_bf", [D, 2560], BF16).ap()
    wfo_bf = nc.dram_tensor("wfo_bf", [512, D], BF16).ap()
    wpo_bf = nc.dram_tensor("wpo_bf", [512, D], BF16).ap()
    wout_bf = nc.dram_tensor("wout_bf", [D, D], BF16).ap()
    wpg_bf = nc.dram_tensor("wpg_bf", [4, 128, 128], BF16).ap()

    S = Sched(nc)
    top = ExitStack()

    def sb(es, name, shape, dt):
        return es.enter_context(nc.sbuf_tensor("s_" + name, list(shape), dt))

    def ps(es, name, shape, dt):
        return es.enter_context(nc.psum_tensor("p_" + name, list(shape), dt))

    ident = sb(top, "ident", [128, 128], BF16)
    colv = sb(top, "colv", [128, 70], F32)
    acol = sb(top, "acol", [128, 2, 4, 8], F32)
    rowg2 = sb(top, "rowg2", [128, 2, D], F32)
    mid = ExitStack()
    rowg1 = sb(mid, "rowg1", [128, 2, D], F32)
    rowg_l = [rowg1, rowg2]
    b_ident, b_colv, b_acol, b_rowg = Buf(), Buf(), Buf(), Buf()
    b_zf, b_x1 = Buf(), Buf()
    b_wscr = Buf()
    for dst_, src_ in ((win_bf, w_in[:, 512:3072]), (wfo_bf, w_fo), (wpo_bf, w_po), (wout_bf, w_out), (wpg_bf, w_pg),
                       (wg_bf, w_gate), (wu_bf, w_up), (wd_bf, w_down)):
        S.dma("pool", dst_, src_, writes=[b_wscr], partial=True)
    S.dma("sp", ident[:], ident_d, writes=[b_ident])
    S.dma("sp", colv[:], colv_d, writes=[b_colv])
    out_toks = []

    def act_rstd(dst, src, b_dst, b_src):
        S.op("act", lambda e: e.activation(out=dst, in_=src, func=AF.Ln, scale=1.0 / D, bias=EPS),
             reads=[b_src], writes=[b_dst])
        S.op("act", lambda e: e.activation(out=dst, in_=dst, func=AF.Exp, scale=-0.5),
             reads=[b_dst], writes=[b_dst])

    with ExitStack() as es:
        stg = [sb(es, "p0_stg%d" % i, [128, 8, 512], F32) for i in range(2)]
        wbf = [sb(es, "p0_wbf%d" % i, [128, 8, 512], BF16) for i in range(2)]
        b_stg = [Buf(), Buf()]
        b_wbf = [Buf(), Buf()]
        scb = sb(es, "p0_sc", [128, 8, 2], BF16)
        screp = sb(es, "p0_screp", [128, 2, 8, 128], BF16)
        rowv = sb(es, "p0_rowv", [128, 4, D], F32)
        adac = sb(es, "p0_adac", [128, 4, 8, 2], F32)
        b_scb, b_screp, b_rowv, b_adac = Buf(), Buf(), Buf(), Buf()
        pc = ps(es, "p0_pc", [128, 4, 2], F32)
        pr = [ps(es, "p0_pr%d" % i, [128, 512], F32) for i in range(2)]
        b_pc = Buf()
        b_pr = [Buf(), Buf()]
        S.dma("sp", rowv[:], rowv_d, writes=[b_rowv])
        for b in range(2):
            S.op("act", lambda e, b=b: e.activation(out=scb[:, :, b], in_=colv[:, 8 * b:8 * b + 8], func=AF.Silu),
                 reads=[b_colv], writes=[b_scb], partial=True)
        for b in range(2):
            for k in range(8):
                S.op("dve", lambda e, b=b, k=k: e.tensor_copy(
                    out=screp[:, b, k, :], in_=scb[:, k, b:b + 1].to_broadcast([128, 128])),
                    reads=[b_scb], writes=[b_screp], partial=True)
        for pc_i in range(12):
            i = pc_i % 2
            S.dma("sp", stg[i][:], w_ada[:, 512 * pc_i:512 * pc_i + 512].rearrange("(k p) c -> p k c", p=128),
                  writes=[b_stg[i]])
            S.op("dve", lambda e, i=i: e.tensor_copy(out=wbf[i][:], in_=stg[i][:]),
                 reads=[b_stg[i]], writes=[b_wbf[i]])
            vec = pc_i // 2
            half = pc_i % 2
            if vec in (2, 5):
                gi = 0 if vec == 2 else 1
                for b in range(2):
                    for k in range(8):
                        S.op("pe", lambda e, b=b, k=k, i=i: e.matmul(
                            pr[b][:], lhsT=screp[:, b, k, :], rhs=wbf[i][:, k, :], start=(k == 0), stop=(k == 7)),
                            reads=[b_screp, b_wbf[i]], writes=[b_pr[b]], partial=True)
                    dst = rowg_l[gi][:, b, 512 * half:512 * half + 512]
                    S.op("dve", lambda e, b=b, dst=dst, gi=gi, half=half: e.tensor_tensor(
                        out=dst, in0=pr[b][:], in1=rowv[:, 2 + gi, 512 * half:512 * half + 512], op=ALU.add),
                        reads=[b_pr[b], b_rowv], writes=[b_rowg], partial=True)
                    S.op("dve", lambda e, dst=dst, gi=gi, half=half: e.tensor_tensor(
                        out=dst, in0=dst, in1=rowv[:, gi, 512 * half:512 * half + 512], op=ALU.mult),
                        reads=[b_rowv, b_rowg], writes=[b_rowg], partial=True)
            else:
                vi = {0: 0, 1: 1, 3: 2, 4: 3}[vec]
                for c in range(4):
                    for k in range(8):
                        S.op("pe", lambda e, c=c, k=k, i=i: e.matmul(
                            pc[:, c, :], lhsT=wbf[i][:, k, 128 * c:128 * c + 128], rhs=scb[:, k, :],
                            start=(k == 0), stop=(k == 7)),
                            reads=[b_scb, b_wbf[i]], writes=[b_pc], partial=True)
                S.op("dve", lambda e, vi=vi, half=half: e.tensor_copy(
                    out=adac[:, vi, 4 * half:4 * half + 4, :], in_=pc[:]),
                    reads=[b_pc], writes=[b_adac], partial=True)
        for b in range(2):
            for si, (gcol, shv, scv) in enumerate(((16, 0, 1), (24, 2, 3))):
                bsh = 32 + 16 * si
                bsc = 40 + 16 * si
                a_dst = acol[:, b, 2 * si, :]
                b_dst_ = acol[:, b, 2 * si + 1, :]
                S.op("dve", lambda e, a_dst=a_dst, scv=scv, b=b, bsc=bsc: e.tensor_tensor(
                    out=a_dst, in0=adac[:, scv, :, b], in1=colv[:, bsc:bsc + 8], op=ALU.add),
                    reads=[b_adac, b_colv], writes=[b_acol], partial=True)
                S.op("dve", lambda e, a_dst=a_dst, gcol=gcol: e.scalar_tensor_tensor(
                    out=a_dst, in0=a_dst, scalar=1.0, in1=colv[:, gcol:gcol + 8], op0=ALU.add, op1=ALU.mult),
                    reads=[b_acol, b_colv], writes=[b_acol], partial=True)
                S.op("dve", lambda e, b_dst_=b_dst_, shv=shv, b=b, bsh=bsh: e.tensor_tensor(
                    out=b_dst_, in0=adac[:, shv, :, b], in1=colv[:, bsh:bsh + 8], op=ALU.add),
                    reads=[b_adac, b_colv], writes=[b_acol], partial=True)
        S.barrier()
        S.emit()

    MAGIC = 0x5F3759DF
    I32 = mybir.dt.int32

    def pipeline(stages, n):
        ns = len(stages)
        for step in range(n + ns - 1):
            for si in reversed(range(ns)):
                t = step - si
                if 0 <= t < n:
                    stages[si](t)

    def dve_rstd(wk, b_wk, srcs, b_src, out, b_out):
        a, yv, tv = wk[:, 0:1], wk[:, 1:2], wk[:, 2:3]
        if len(srcs) == 2:
            S.op("dve", lambda e: e.tensor_tensor(out=a, in0=srcs[0], in1=srcs[1], op=ALU.add),
                 reads=[b_src], writes=[b_wk])
            S.op("dve", lambda e: e.tensor_scalar(out=a, in0=a, scalar1=1.0 / D, scalar2=EPS, op0=ALU.mult, op1=ALU.add),
                 reads=[b_wk], writes=[b_wk])
        else:
            S.op("dve", lambda e: e.tensor_scalar(out=a, in0=srcs[0], scalar1=1.0 / D, scalar2=EPS, op0=ALU.mult,
                                                  op1=ALU.add), reads=[b_src], writes=[b_wk])
        S.op("dve", lambda e: e.tensor_single_scalar(out=yv.bitcast(I32), in_=a.bitcast(I32), scalar=1,
                                                     op=ALU.arith_shift_right), reads=[b_wk], writes=[b_wk])
        S.op("dve", lambda e: e.tensor_scalar(out=yv.bitcast(I32), in0=yv.bitcast(I32), scalar1=-1, scalar2=MAGIC,
                                              op0=ALU.mult, op1=ALU.add), reads=[b_wk], writes=[b_wk])
        for it in range(2):
            S.op("dve", lambda e: e.scalar_tensor_tensor(out=tv, in0=yv, scalar=a, in1=yv, op0=ALU.mult, op1=ALU.mult),
                 reads=[b_wk], writes=[b_wk])
            S.op("dve", lambda e: e.tensor_scalar(out=tv, in0=tv, scalar1=-0.5, scalar2=1.5, op0=ALU.mult, op1=ALU.add),
                 reads=[b_wk], writes=[b_wk])
            if it == 0:
                S.op("dve", lambda e: e.tensor_tensor(out=yv, in0=yv, in1=tv, op=ALU.mult), reads=[b_wk], writes=[b_wk])
            else:
                S.op("dve", lambda e: e.tensor_tensor(out=out, in0=yv, in1=tv, op=ALU.mult),
                     reads=[b_wk], writes=[b_out])

    def transposes(xn, b_xn, pT, b_pT):
        for k in range(8):
            S.op("pe", lambda e, k=k: e.transpose(out=pT[:, 128 * k:128 * k + 128], in_=xn[:, 128 * k:128 * k + 128],
                                                  identity=ident[:]),
                 reads=[b_xn, b_ident], writes=[b_pT], partial=True)

    def evacs(pT, b_pT, hT_dst, b_hT, bi, ni):
        for k in range(8):
            S.op("dve", lambda e, k=k: e.tensor_scalar(
                out=hT_dst(k), in0=pT[:, 128 * k:128 * k + 128],
                scalar1=acol[:, bi, 2 * ni, k:k + 1], scalar2=acol[:, bi, 2 * ni + 1, k:k + 1],
                op0=ALU.mult, op1=ALU.add),
                reads=[b_pT, b_acol], writes=[b_hT], partial=True)

    with ExitStack() as es:
        wf = sb(es, "a_wf", [128, 8, 512], BF16)
        wfa = [sb(es, "a_wfa%d" % i, [128, 8, 512], BF16) for i in range(2)]
        fixv = sb(es, "a_fixv", [128, 4, 2], F32)
        b_wf, b_fixv = Buf(), Buf()
        CSs = sb(es, "a_cs", [128, 2, 128], BF16)
        b_CS = Buf()
        S.dma("sp", CSs[:], CS_d, writes=[b_CS])
        with ExitStack() as es2:
            stg = sb(es2, "a_stg", [128, 8, 512], F32)
            b_st = Buf()
            S.dma("sp", stg[:], w_in[:, 0:512].rearrange("(k p) c -> p k c", p=128), writes=[b_st])
            S.op("dve", lambda e: e.tensor_copy(out=wf[:], in_=stg[:]), reads=[b_st], writes=[b_wf])
            bcol = sb(es2, "a_bcol", [128, 2, 8], BF16)
            b_bcol = Buf()
            S.op("dve", lambda e: e.tensor_copy(out=bcol[:], in_=acol[:, :, 1, :]), reads=[b_acol], writes=[b_bcol])
            pbw = ps(es2, "a_pbw", [128, 4, 2], F32)
            b_pbw = Buf()
            for b in range(2):
                for k in range(8):
                    S.op("dve", lambda e, b=b, k=k: e.tensor_scalar(
                        out=wfa[b][:, k, :], in0=stg[:, k, :], scalar1=acol[:, b, 0, k:k + 1], scalar2=None,
                        op0=ALU.mult), reads=[b_st, b_acol], writes=[b_wf], partial=True)
                for j in range(4):
                    for k in range(8):
                        S.op("pe", lambda e, b=b, j=j, k=k: e.matmul(
                            pbw[:, j, b:b + 1], lhsT=wf[:, k, 128 * j:128 * j + 128], rhs=bcol[:, b, k:k + 1],
                            start=(k == 0), stop=(k == 7)), reads=[b_wf, b_bcol], writes=[b_pbw], partial=True)
                S.op("dve", lambda e, b=b: e.tensor_scalar(
                    out=fixv[:, :, b], in0=pbw[:, :, b], scalar1=colv[:, 68 + b:69 + b],
                    scalar2=float(SP if b == 0 else SS), op0=ALU.mult, op1=ALU.mult),
                    reads=[b_pbw, b_colv], writes=[b_fixv], partial=True)
            S.barrier()
            S.emit()
        NX = 4
        xts = [sb(es, "a_xt%d" % i, [128, D], F32) for i in range(NX)]
        b_xts = [Buf() for _ in range(NX)]
        junk = sb(es, "a_junk", [128, D], BF16)
        ND = 3
        xns = [sb(es, "a_xn%d" % i, [128, D], BF16) for i in range(ND)]
        b_xns = [Buf() for _ in range(ND)]
        hTs = [sb(es, "a_hT%d" % i, [128, 8, 128], BF16) for i in range(ND)]
        b_hTs = [Buf() for _ in range(ND)]
        us = [sb(es, "a_u%d" % i, [128, 512], BF16) for i in range(ND)]
        b_us = [Buf() for _ in range(ND)]
        ssr = [sb(es, "a_ss%d" % i, [128, 2], F32) for i in range(ND)]
        b_ssr = [Buf() for _ in range(ND)]
        b_rsr = [Buf() for _ in range(ND)]
        HT = sb(es, "a_HT", [128, 4 * 2 * 32 * 128], BF16)
        b_HT = Buf()
        Eps = sb(es, "a_Ep", [128, 64], BF16)
        Ess = sb(es, "a_Es", [128, 256], BF16)
        b_E = Buf()
        S.dma("sp", Eps[:], Ep_d, writes=[b_E], partial=True)
        S.dma("sp", Ess[:], Es_d, writes=[b_E], partial=True)
        Tg = [sb(es, "a_T%d" % i, [128, 2, 256], BF16) for i in range(3)]
        b_Tg = [Buf() for _ in range(3)]
        Hg = [sb(es, "a_Hg%d" % i, [128, 2, 512], BF16) for i in range(2)]
        b_Hg = [Buf(), Buf()]
        Pg = [sb(es, "a_Pg%d" % i, [128, 4, 256], BF16) for i in range(2)]
        b_Pg = [Buf(), Buf()]
        Zf = sb(es, "a_Zf", [128, 4, NOWN], BF16)
        b_Zf = Buf()
        pTs = [ps(es, "a_pT%d" % i, [128, D], BF16) for i in range(2)]
        b_pTs = [Buf(), Buf()]
        pUs = [ps(es, "a_pU%d" % i, [128, 512], F32) for i in range(2)]
        b_pUs = [Buf(), Buf()]
        p1s = [ps(es, "a_p1%d" % i, [128, 1024], F32) for i in range(2)]
        b_p1s = [Buf(), Buf()]

        for seq in range(2):
            if seq == 0:
                row0, Sq, N1, NE, NG = 0, SP, 128, 64, 32
                E_sb = Eps
                HTv = HT[:].rearrange("p (s c) -> p s c", s=128)
            else:
                row0, Sq, N1, NE, NG = SP, SS, 16, 256, 16
                E_sb = Ess
                HTv = HT[:, 0:4 * 2 * 16 * 128].rearrange("p (j r g s k) -> p j r g s k", j=4, r=2, g=16, s=16, k=8)
            xseq = xa[row0:row0 + Sq, :].rearrange("(s2 s1) d -> s1 s2 d", s1=N1)

            def a_s0(t):
                S.dma("sp", xts[t % NX][:], xseq[t], writes=[b_xts[t % NX]])

            def a_s1(t):
                xt, bx = xts[t % NX], b_xts[t % NX]
                ss, rs = ssr[t % ND][:, 0:1], ssr[t % ND][:, 1:2]
                S.op("act", lambda e: e.activation(out=junk[:], in_=xt[:], func=AF.Square, accum_out=ss),
                     reads=[bx], writes=[b_ssr[t % ND]])
                act_rstd(rs, ss, b_rsr[t % ND], b_ssr[t % ND])

            def a_s2(t):
                xt, bx = xts[t % NX], b_xts[t % NX]
                rs = ssr[t % ND][:, 1:2]
                xn = xns[t % ND]
                S.op("pool", lambda e: e.tensor_tensor(out=xn[:], in0=xt[:], in1=rs.to_broadcast([128, D]), op=ALU.mult),
                     reads=[bx, b_rsr[t % ND]], writes=[b_xns[t % ND]])

            def a_s3(t):
                transposes(xns[t % ND], b_xns[t % ND], pTs[t % 2], b_pTs[t % 2])

            def a_s4(t, seq=seq):
                hT, pT = hTs[t % ND], pTs[t % 2]
                S.op("dve", lambda e: e.tensor_copy(out=hT[:].rearrange("p k t -> p (k t)"), in_=pT[:]),
                     reads=[b_pTs[t % 2]], writes=[b_hTs[t % ND]])

            def a_s5(t, seq=seq):
                hT, pU = hTs[t % ND], pUs[t % 2]
                for k in range(8):
                    S.op("pe", lambda e, k=k: e.matmul(pU[:], lhsT=hT[:, k, :], rhs=wfa[seq][:, k, :],
                                                       start=(k == 0), stop=(k == 7)),
                         reads=[b_hTs[t % ND], b_wf], writes=[b_pUs[t % 2]], partial=True)

            def a_s6(t):
                u, pU = us[t % ND], pUs[t % 2]
                S.op("act", lambda e: e.activation(out=u[:], in_=pU[:], func=AF.Copy),
                     reads=[b_pUs[t % 2]], writes=[b_us[t % ND]])

            def a_s7(t, E_sb=E_sb, NE=NE):
                u, p1 = us[t % ND], p1s[t % 2]
                for j in range(4):
                    S.op("pe", lambda e, j=j: e.matmul(p1[:, NE * j:NE * j + NE], lhsT=u[:, 128 * j:128 * j + 128],
                                                       rhs=E_sb[:], start=True, stop=True),
                         reads=[b_us[t % ND], b_E], writes=[b_p1s[t % 2]], partial=True)

            def a_s8(t, seq=seq, NE=NE, HTv=HTv):
                p1 = p1s[t % 2]
                p1v = p1[:, 0:4 * NE].rearrange("p (j e) -> p j e", j=4)
                for r in range(2):
                    if seq == 0:
                        if r == 1:
                            continue
                        dst = HTv[:, t, :]
                        src = p1[:, 0:256]
                    else:
                        dst = HTv[:, :, r, :, t, :]
                        src = p1v.rearrange("p j (g k r) -> p j r g k", r=2, k=8)[:, :, r, :, :]
                    S.op("dve", lambda e, dst=dst, src=src: e.tensor_copy(out=dst, in_=src),
                         reads=[b_p1s[t % 2]], writes=[b_HT], partial=True)

            pipeline([a_s0, a_s1, a_s2, a_s3, a_s4, a_s5, a_s6, a_s7, a_s8], N1)

            scale = 1.0 / float(np.sqrt(Sq * 128.0))
            T_d = Tp_d if seq == 0 else Ts_d

            def g_s0(g, T_d=T_d):
                S.dma("sp", Tg[g % 3][:], T_d[g], writes=[b_Tg[g % 3]])

            def g_s1(g, seq=seq, HTv=HTv):
                pT = pTs[g % 2]
                for r in range(2):
                    for j in range(4):
                        if seq == 0:
                            src = HTv[:, :, j * 64 + r * 32 + g]
                        else:
                            src = HTv[:, j, r, g, :, :].rearrange("p s k -> p (s k)")
                        c0 = 512 * r + 128 * j
                        S.op("pe", lambda e, src=src, c0=c0: e.transpose(out=pT[:, c0:c0 + 128], in_=src,
                                                                         identity=ident[:]),
                             reads=[b_HT, b_ident], writes=[b_pTs[g % 2]], partial=True)

            def g_s2(g):
                pT, H = pTs[g % 2], Hg[g % 2]
                S.op("act", lambda e: e.activation(out=H[:].rearrange("p r c -> p (r c)"), in_=pT[:], func=AF.Copy),
                     reads=[b_pTs[g % 2]], writes=[b_Hg[g % 2]])

            def g_s3(g):
                H, T, p1 = Hg[g % 2], Tg[g % 3], p1s[g % 2]
                for j in range(4):
                    for r in range(2):
                        S.op("pe", lambda e, j=j, r=r: e.matmul(
                            p1[:, 256 * j:256 * j + 256], lhsT=H[:, r, 128 * j:128 * j + 128],
                            rhs=T[:, r, :], start=(r == 0), stop=(r == 1)),
                            reads=[b_Hg[g % 2], b_Tg[g % 3]], writes=[b_p1s[g % 2]], partial=True)

            def g_s4(g, seq=seq):
                P, p1 = Pg[g % 2], p1s[g % 2]
                S.op("dve", lambda e: e.tensor_copy(out=P[:].rearrange("p j c -> p (j c)"), in_=p1[:]),
                     reads=[b_p1s[g % 2]], writes=[b_Pg[g % 2]])
                if g == 0:
                    S.op("dve", lambda e: e.tensor_tensor(
                        out=P[:, :, 0:1], in0=p1[:].rearrange("p (j c) -> p j c", j=4)[:, :, 0:1],
                        in1=fixv[:, :, seq:seq + 1], op=ALU.add),
                        reads=[b_p1s[g % 2], b_fixv, b_Pg[g % 2]], writes=[b_Pg[g % 2]], partial=True)

            def g_s5(g):
                P, pU = Pg[g % 2], pUs[g % 2]
                for j in range(4):
                    for r in range(2):
                        S.op("pe", lambda e, j=j, r=r: e.matmul(
                            pU[:, 128 * j:128 * j + 128], lhsT=CSs[:, r, :], rhs=P[:, j, 128 * r:128 * r + 128],
                            start=(r == 0), stop=(r == 1)),
                            reads=[b_Pg[g % 2], b_CS], writes=[b_pUs[g % 2]], partial=True)

            def g_s6(g, seq=seq, scale=scale):
                pU = pUs[g % 2]
                if seq == 0:
                    dst = Zf[:, :, 0:4096].rearrange("p j (k g) -> p j g k", g=32)[:, :, g, :]
                    src = pU[:].rearrange("p (j k) -> p j k", j=4)
                else:
                    dst = Zf[:, :, 4096:NOWN].rearrange("p j (k b s) -> p j b k s", b=16, s=8)[:, :, g, :, :]
                    src = pU[:].rearrange("p (j k s) -> p j k s", j=4, s=8)
                S.op("dve", lambda e: e.tensor_scalar(out=dst, in0=src, scalar1=scale, scalar2=None, op0=ALU.mult),
                     reads=[b_pUs[g % 2]], writes=[b_Zf], partial=True)

            pipeline([g_s0, g_s1, g_s2, g_s3, g_s4, g_s5, g_s6], NG)
        for j in range(4):
            S.dma("sp", zf_d[j], Zf[:, j, :], reads=[b_Zf], writes=[b_zf], partial=True)
        S.barrier()
        S.emit()

    def load_weight(dst, src, K, cols, stgs, b_stgs, b_dst, ctr):
        step = 4096 // K
        for c0 in range(0, cols, step):
            cw = min(step, cols - c0)
            i = ctr[0] % 2
            ctr[0] += 1
            view = stgs[i][:, 0:K * cw].rearrange("p (k c) -> p k c", k=K)
            S.dma("sp", view, src[:, c0:c0 + cw].rearrange("(k p) c -> p k c", p=128), writes=[b_stgs[i]])
            S.op("dve" if i == 0 else "pool", lambda e, view=view, c0=c0, cw=cw: e.tensor_copy(
                out=dst[:, :, c0:c0 + cw], in_=view), reads=[b_stgs[i]], writes=[b_dst], partial=True)

    def backend(zh, b_zh, junk, ssm, b_ssm, wk, b_wk, gg, tmp, b_tmp, xres, b_xres, dst_rows, b_dst):
        for n in range(2):
            S.op("act", lambda e, n=n: e.activation(out=junk[:, 0:512], in_=zh[n], func=AF.Square,
                                                    accum_out=ssm[:, n:n + 1]),
                 reads=[b_zh[n]], writes=[b_ssm], partial=(n == 1))
        dve_rstd(wk, b_wk, [ssm[:, 0:1], ssm[:, 1:2]], b_ssm, ssm[:, 3:4], b_ssm)
        for n in range(2):
            S.op("dve", lambda e, n=n: e.scalar_tensor_tensor(
                out=tmp[:, 512 * n:512 * n + 512], in0=zh[n], scalar=ssm[:, 3:4],
                in1=gg[:, 512 * n:512 * n + 512], op0=ALU.mult, op1=ALU.mult),
                reads=[b_zh[n], b_ssm, b_rowg], writes=[b_tmp], partial=(n == 1))
        S.op("pool", lambda e: e.tensor_tensor(out=tmp[:], in0=tmp[:], in1=xres[:], op=ALU.add),
             reads=[b_tmp, b_xres], writes=[b_tmp])
        return S.dma("sp", dst_rows, tmp[:], reads=[b_tmp], writes=[b_dst], partial=True)

    with ExitStack() as es:
        wi = sb(es, "b1_wi", [128, 8, 2560], BF16)
        wfo = sb(es, "b1_wfo", [128, 4, D], BF16)
        wpo = sb(es, "b1_wpo", [128, 4, D], BF16)
        wou = sb(es, "b1_wou", [128, 8, D], BF16)
        wpg = sb(es, "b1_wpg", [128, 4, 128], BF16)
        b_w = Buf()
        for c0 in range(0, 2560, 640):
            S.dma("sp", wi[:, :, c0:c0 + 640], win_bf[:, c0:c0 + 640].rearrange("(k p) c -> p k c", p=128),
                  reads=[b_wscr], writes=[b_w], partial=True)
        S.dma("sp", wfo[:], wfo_bf.rearrange("(k p) c -> p k c", p=128), reads=[b_wscr], writes=[b_w], partial=True)
        S.dma("sp", wpo[:], wpo_bf.rearrange("(k p) c -> p k c", p=128), reads=[b_wscr], writes=[b_w], partial=True)
        for c0 in range(0, D, 512):
            S.dma("sp", wou[:, :, c0:c0 + 512], wout_bf[:, c0:c0 + 512].rearrange("(k p) c -> p k c", p=128),
                  reads=[b_wscr], writes=[b_w], partial=True)
        S.dma("sp", wpg[:], wpg_bf.rearrange("g c d -> c g d"), reads=[b_wscr], writes=[b_w], partial=True)
        S.barrier()
        S.emit()
        NX = 2
        xts = [sb(es, "b1_xt%d" % i, [128, D], F32) for i in range(NX)]
        b_xts = [Buf() for _ in range(NX)]
        junk = sb(es, "b1_junk", [128, D], BF16)
        NXN = 3
        xns = [sb(es, "b1_xn%d" % i, [128, D], BF16) for i in range(NXN)]
        b_xns = [Buf() for _ in range(NXN)]
        ssr = [sb(es, "b1_ss%d" % i, [128, 8], F32) for i in range(NXN)]
        b_ssr = [Buf() for _ in range(NXN)]
        b_rsr = [Buf() for _ in range(NXN)]
        b_wkf = [Buf() for _ in range(NXN)]
        hTs = [sb(es, "b1_hT%d" % i, [128, 8, 768], BF16) for i in range(2)]
        b_hTs = [Buf(), Buf()]
        up = sb(es, "b1_up", [128, 4, 768], F32)
        b_up = Buf()
        A2 = sb(es, "b1_A2", [128, 16, 48], F32)
        A4 = sb(es, "b1_A4", [128, 16, 48], F32)
        A8 = sb(es, "b1_A8", [128, 16, 48], F32)
        win = sb(es, "b1_win", [128, 16, 32], F32)
        b_A2, b_A4, b_A8, b_win = Buf(), Buf(), Buf(), Buf()
        dbf = sb(es, "b1_d", [128, 4, 512], BF16)
        b_d = Buf()
        pm1 = sb(es, "b1_pm", [128, 4, 512], BF16)
        pms = [pm1, pm1]
        b_pm1 = Buf()
        b_pms = [b_pm1, b_pm1]
        mskt = sb(es, "b1_msk", [128, 768], F32)
        b_msk = Buf()
        icnt = sb(es, "b1_icn", [128, 4, 512], F32)
        b_icn = Buf()
        zft = sb(es, "b1_zf", [128, 4, 512], BF16)
        b_zft = Buf()
        gfs = sb(es, "b1_gf", [128, 512], F32)
        gps = sb(es, "b1_gp", [128, 512], F32)
        t1 = sb(es, "b1_t1", [128, 512], F32)
        t2 = sb(es, "b1_t2", [128, 512], F32)
        b_gf, b_gp, b_t1, b_t2 = Buf(), Buf(), Buf(), Buf()
        mg = sb(es, "b1_mg", [128, 8, 512], BF16)
        b_mg = Buf()
        xres = sb(es, "b1_xres", [128, D], F32)
        tmp = sb(es, "b1_tmp", [128, D], F32)
        b_xres, b_tmp = Buf(), Buf()
        ssm = sb(es, "b1_ssm", [128, 8], F32)
        b_ssm, b_wkb = Buf(), Buf()
        pT = ps(es, "b1_pT", [128, D], BF16)
        b_pT = Buf()
        pUM = ps(es, "b1_pUM", [128, 1024], F32)
        b_pUMh = [Buf(), Buf()]
        pG = [ps(es, "b1_pG%d" % i, [128, 512], F32) for i in range(4)]
        b_pG = [Buf() for _ in range(4)]

        fe_ctr = [0]
        def b1_fea(mt, i):
            tcn = fe_ctr[0]
            fe_ctr[0] += 1
            ix, i3 = tcn % NX, tcn % NXN
            r0 = 768 * mt + 128 * i
            xt, bx = xts[ix], b_xts[ix]
            S.dma("sp", xt[:], xb[r0:r0 + 128, :], writes=[bx])
            sst = ssr[i3]
            S.op("act", lambda e: e.activation(out=junk[:], in_=xt[:], func=AF.Square, accum_out=sst[:, 0:1]),
                 reads=[bx], writes=[b_ssr[i3]])
            dve_rstd(sst[:, 4:8], b_wkf[i3], [sst[:, 0:1]], b_ssr[i3], sst[:, 1:2], b_rsr[i3])
            xn = xns[i3]
            S.op("pool", lambda e: e.tensor_tensor(out=xn[:], in0=xt[:], in1=sst[:, 1:2].to_broadcast([128, D]),
                                                   op=ALU.mult), reads=[bx, b_rsr[i3]], writes=[b_xns[i3]])
            return i3

        def b1_feb(mt, i, i3):
            hT = hTs[mt % 2]
            bi = 0 if mt < 8 else 1
            transposes(xns[i3], b_xns[i3], pT, b_pT)
            evacs(pT, b_pT, (lambda k: hT[:, k, 128 * i:128 * i + 128]), b_hTs[mt % 2], bi, 0)

        def b1_up(mt, g, alt=False):
            hT = hTs[mt % 2]
            if alt:
                banks = [(pG[0][:, 0:384], b_pG[0]), (pG[1][:, 0:384], b_pG[1])]
            else:
                banks = [(pUM[:, 0:384], b_pUMh[0]), (pUM[:, 512:896], b_pUMh[1])]
            if g == 0:
                S.dma("sp", mskt[:], msk_d[:, 768 * mt:768 * mt + 768], writes=[b_msk])
                S.dma("sp", icnt[:], icn_d[:, :, 512 * mt:512 * mt + 512], writes=[b_icn])
            for h in range(2):
                for k in range(8):
                    S.op("pe", lambda e, h=h, k=k: e.matmul(
                        banks[h][0], lhsT=wi[:, k, 128 * g:128 * g + 128],
                        rhs=hT[:, k, 384 * h:384 * h + 384], start=(k == 0), stop=(k == 7)),
                        reads=[b_w, b_hTs[mt % 2]], writes=[banks[h][1]], partial=True)
            for h in range(2):
                S.op("dve", lambda e, h=h: e.tensor_tensor(
                    out=up[:, g, 384 * h:384 * h + 384], in0=banks[h][0],
                    in1=mskt[:, 384 * h:384 * h + 384], op=ALU.mult),
                    reads=[banks[h][1], b_msk], writes=[b_up], partial=True)

        def b1_pool(mt, g):
            U = up[:, g, :].rearrange("p (r c) -> p r c", c=48)

            def tt(out, a, b, rd, wr, op=ALU.add):
                S.op("pool", lambda e: e.tensor_tensor(out=out, in0=a, in1=b, op=op), reads=rd, writes=wr)
            if g == 0:
                tt(win[:], U[:, :, 7:39], U[:, :, 8:40], [b_up], [b_win])
            else:
                tt(A2[:, :, 0:47], U[:, :, 0:47], U[:, :, 1:48], [b_up], [b_A2])
                if g == 1:
                    tt(win[:], A2[:, :, 6:38], A2[:, :, 8:40], [b_A2], [b_win])
                else:
                    tt(A4[:, :, 0:45], A2[:, :, 0:45], A2[:, :, 2:47], [b_A2], [b_A4])
                    if g == 2:
                        tt(win[:], A4[:, :, 4:36], A4[:, :, 8:40], [b_A4], [b_win])
                    else:
                        tt(A8[:, :, 0:41], A4[:, :, 0:41], A4[:, :, 4:45], [b_A4], [b_A8])
                        tt(win[:], A8[:, :, 0:32], A8[:, :, 8:40], [b_A8], [b_win])
            tt(win[:], win[:], icnt[:, g, :].rearrange("p (r c) -> p r c", c=32), [b_win, b_icn], [b_win], op=ALU.mult)
            tt(dbf[:, g, :].rearrange("p (r c) -> p r c", c=32), win[:], U[:, :, 8:40], [b_win, b_up], [b_d],
               op=ALU.subtract)

        def b1_pg(mt):
            pm = pms[mt % 2]
            for g in range(4):
                ob, bb = pUM[:, 512 * (g % 2):512 * (g % 2) + 512], b_pUMh[g % 2]
                S.op("pe", lambda e, g=g, ob=ob: e.matmul(ob, lhsT=wpg[:, g, :], rhs=dbf[:, g, :], start=True, stop=True),
                     reads=[b_w, b_d], writes=[bb], partial=True)
                S.op("dve", lambda e, g=g, ob=ob: e.tensor_scalar(out=pm[:, g, :], in0=ob, scalar1=colv[:, 64 + g:65 + g],
                                                                  scalar2=None, op0=ALU.mult),
                     reads=[bb, b_colv], writes=[b_pms[mt % 2]], partial=True)

        def b1_gc(mt, c):
            hT, pm = hTs[mt % 2], pms[mt % 2]
            if c == 0:
                S.dma("sp", zft[:], zf_d[:, :, 512 * mt:512 * mt + 512].rearrange("j p t -> p j t"),
                      reads=[b_zf], writes=[b_zft])
            for pb, col0 in ((0, 512 + 128 * c), (1, 1536 + 128 * c)):
                for k in range(8):
                    S.op("pe", lambda e, pb=pb, col0=col0, k=k: e.matmul(
                        pG[pb][:].rearrange("p (r c) -> p r c", c=32), lhsT=wi[:, k, col0:col0 + 128],
                        rhs=hT[:, k, :].rearrange("p (r c) -> p r c", c=48)[:, :, 8:40],
                        start=(k == 0), stop=(k == 7)),
                        reads=[b_w, b_hTs[mt % 2]], writes=[b_pG[pb]], partial=True)
                gdst, b_g = (gfs, b_gf) if pb == 0 else (gps, b_gp)
                S.op("act", lambda e, gdst=gdst, pb=pb: e.activation(out=gdst[:], in_=pG[pb][:], func=AF.Sigmoid),
                     reads=[b_pG[pb]], writes=[b_g])
            for k in range(4):
                S.op("pe", lambda e, k=k: e.matmul(pG[3][:], lhsT=wpo[:, k, 128 * c:128 * c + 128],
                                                   rhs=pm[:, k, :], start=(k == 0), stop=(k == 3)),
                     reads=[b_w, b_pms[mt % 2]], writes=[b_pG[3]], partial=True)
            for k in range(4):
                S.op("pe", lambda e, k=k: e.matmul(pG[2][:], lhsT=wfo[:, k, 128 * c:128 * c + 128],
                                                   rhs=zft[:, k, :], start=(k == 0), stop=(k == 3)),
                     reads=[b_w, b_zft], writes=[b_pG[2]], partial=True)
            S.op("dve", lambda e: e.tensor_tensor(out=t1[:], in0=pG[2][:], in1=gfs[:], op=ALU.mult),
                 reads=[b_pG[2], b_gf], writes=[b_t1])
            S.op("dve", lambda e: e.tensor_tensor(out=t2[:], in0=pG[3][:], in1=gps[:], op=ALU.mult),
                 reads=[b_pG[3], b_gp], writes=[b_t2])
            S.op("pool", lambda e: e.tensor_tensor(out=mg[:, c, :], in0=t1[:], in1=t2[:], op=ALU.add),
                 reads=[b_t1, b_t2], writes=[b_mg], partial=True)

        def b1_wo(mt, s):
            bi = 0 if mt < 8 else 1
            if s % 2 == 0:
                zh, bz = [pUM[:, 0:512], pUM[:, 512:1024]], b_pUMh
            else:
                zh, bz = [pG[2][:], pG[3][:]], [b_pG[2], b_pG[3]]
            for n in range(2):
                for c in range(8):
                    S.op("pe", lambda e, n=n, c=c: e.matmul(
                        zh[n], lhsT=mg[:, c, 128 * s:128 * s + 128], rhs=wou[:, c, 512 * n:512 * n + 512],
                        start=(c == 0), stop=(c == 7)),
                        reads=[b_w, b_mg], writes=[bz[n]], partial=True)
            for rr in range(4):
                run = 16 * mt + 4 * s + rr
                S.dma("sp", xres[32 * rr:32 * rr + 32, :], xb[48 * run + 8:48 * run + 40, :],
                      writes=[b_xres], partial=True)
            r0 = 512 * mt + 128 * s
            backend(zh, bz, junk, ssm, b_ssm, ssm[:, 4:8], b_wkb, rowg1[:, bi, :], tmp, b_tmp, xres, b_xres,
                    x1_d[r0:r0 + 128, :], b_x1)

        def b1_fe_sched(mt):
            slots = [[] for _ in range(8)]
            ids = {}

            def mk_a(i):
                return lambda: ids.__setitem__(i, b1_fea(mt, i))

            def mk_b(i):
                return lambda: b1_feb(mt, i, ids[i])
            plan = {0: [("a", 0), ("a", 1)], 1: [("b", 0), ("a", 2)], 2: [("b", 1), ("a", 3)], 3: [("b", 2), ("a", 4)],
                    4: [("b", 3), ("a", 5)], 5: [("b", 4)], 6: [("b", 5)]}
            for c, lst in plan.items():
                for kind, i in lst:
                    slots[c].append(mk_a(i) if kind == "a" else mk_b(i))
            return slots

        for slot in b1_fe_sched(0):
            for f in slot:
                f()
        for g in range(4):
            b1_up(0, g)
            b1_pool(0, g)
        for mt in range(NMT):
            b1_pg(mt)
            nxt = b1_fe_sched(mt + 1) if mt + 1 < NMT else [[] for _ in range(8)]
            for c in range(8):
                b1_gc(mt, c)
                for f in nxt[c]:
                    f()
                if mt + 1 < NMT and c >= 6:
                    b1_up(mt + 1, c - 6)
                    b1_pool(mt + 1, c - 6)
            for s in range(4):
                b1_wo(mt, s)
                if mt + 1 < NMT and s < 2:
                    b1_up(mt + 1, 2 + s, alt=True)
                    b1_pool(mt + 1, 2 + s)
        S.barrier()
        S.emit()
    mid.close()

    with ExitStack() as es:
        wg = sb(es, "b2_wg", [128, 8, DFF], BF16)
        wu = sb(es, "b2_wu", [128, 8, DFF], BF16)
        wd = sb(es, "b2_wd", [128, NJ, D], BF16)
        b_w = Buf()
        for c0 in range(0, DFF, 704):
            S.dma("sp", wg[:, :, c0:c0 + 704], wg_bf[:, c0:c0 + 704].rearrange("(k p) c -> p k c", p=128),
                  reads=[b_wscr], writes=[b_w], partial=True)
            S.dma("sp", wu[:, :, c0:c0 + 704], wu_bf[:, c0:c0 + 704].rearrange("(k p) c -> p k c", p=128),
                  reads=[b_wscr], writes=[b_w], partial=True)
        for j0 in range(0, NJ, 6):
            nj = min(6, NJ - j0)
            S.dma("sp", wd[:, j0:j0 + nj, :], wd_bf[128 * j0:128 * (j0 + nj), :].rearrange("(k p) c -> p k c", p=128),
                  reads=[b_wscr], writes=[b_w], partial=True)
        S.barrier()
        S.emit()
        NX = 2
        xts = [sb(es, "b2_xt%d" % i, [128, D], F32) for i in range(NX)]
        b_xts = [Buf() for _ in range(NX)]
        junk = sb(es, "b2_junk", [128, D], BF16)
        xns = [sb(es, "b2_xn%d" % i, [128, D], BF16) for i in range(2)]
        b_xns = [Buf(), Buf()]
        ssr = [sb(es, "b2_ss%d" % i, [128, 8], F32) for i in range(2)]
        b_ssr = [Buf(), Buf()]
        b_rsr = [Buf(), Buf()]
        b_wkf = [Buf(), Buf()]
        hTs = [sb(es, "b2_hT%d" % i, [128, 8, 512], BF16) for i in range(2)]
        b_hTs = [Buf(), Buf()]
        sa = [sb(es, "b2_sa%d" % i, [128, 512], F32) for i in range(2)]
        b_sa = [Buf(), Buf()]
        aT = sb(es, "b2_aT", [128, NJ, 512], BF16)
        b_aT = Buf()
        xres = sb(es, "b2_xres", [128, D], F32)
        tmp = sb(es, "b2_tmp", [128, D], F32)
        b_xres, b_tmp = Buf(), Buf()
        ssm = sb(es, "b2_ssm", [128, 8], F32)
        b_ssm, b_wkb = Buf(), Buf()
        b_y = Buf()
        pT = ps(es, "b2_pT", [128, D], BF16)
        b_pT = Buf()
        pA = [ps(es, "b2_pA%d" % i, [128, 512], F32) for i in range(2)]
        pB = [ps(es, "b2_pB%d" % i, [128, 512], F32) for i in range(2)]
        b_pA = [Buf(), Buf()]
        b_pB = [Buf(), Buf()]
        pZ = ps(es, "b2_pZ", [128, 1024], F32)
        b_pZh = [Buf(), Buf()]
        fe_ctr = [0]

        def b2_fea(mt, s):
            tcn = fe_ctr[0]
            fe_ctr[0] += 1
            ix, i2 = tcn % NX, tcn % 2
            r0 = 512 * mt + 128 * s
            xt, bx = xts[ix], b_xts[ix]
            S.dma("sp", xt[:], x1_d[r0:r0 + 128, :], reads=[b_x1], writes=[bx])
            sst = ssr[i2]
            S.op("act", lambda e: e.activation(out=junk[:], in_=xt[:], func=AF.Square, accum_out=sst[:, 0:1]),
                 reads=[bx], writes=[b_ssr[i2]])
            dve_rstd(sst[:, 4:8], b_wkf[i2], [sst[:, 0:1]], b_ssr[i2], sst[:, 1:2], b_rsr[i2])
            xn = xns[i2]
            S.op("pool", lambda e: e.tensor_tensor(out=xn[:], in0=xt[:], in1=sst[:, 1:2].to_broadcast([128, D]),
                                                   op=ALU.mult), reads=[bx, b_rsr[i2]], writes=[b_xns[i2]])
            return i2

        def b2_feb(mt, s, i2):
            hT = hTs[mt % 2]
            bi = 0 if mt < 8 else 1
            transposes(xns[i2], b_xns[i2], pT, b_pT)
            evacs(pT, b_pT, (lambda k: hT[:, k, 128 * s:128 * s + 128]), b_hTs[mt % 2], bi, 1)

        def b2_fe_sched(mt):
            slots = {}
            ids = {}

            def mk_a(i):
                return lambda: ids.__setitem__(i, b2_fea(mt, i))

            def mk_b(i):
                return lambda: b2_feb(mt, i, ids[i])
            plan = {1: [("a", 0)], 4: [("b", 0), ("a", 1)], 8: [("b", 1), ("a", 2)], 12: [("b", 2), ("a", 3)],
                    16: [("b", 3)]}
            for j, lst in plan.items():
                slots[j] = [mk_a(i) if kind == "a" else mk_b(i) for kind, i in lst]
            return slots

        def b2_gu(mt, j):
            hT = hTs[mt % 2]
            i2 = j % 2
            for k in range(8):
                S.op("pe", lambda e, k=k: e.matmul(pA[i2][:], lhsT=wg[:, k, 128 * j:128 * j + 128],
                                                   rhs=hT[:, k, :], start=(k == 0), stop=(k == 7)),
                     reads=[b_w, b_hTs[mt % 2]], writes=[b_pA[i2]], partial=True)
            for k in range(8):
                S.op("pe", lambda e, k=k: e.matmul(pB[i2][:], lhsT=wu[:, k, 128 * j:128 * j + 128],
                                                   rhs=hT[:, k, :], start=(k == 0), stop=(k == 7)),
                     reads=[b_w, b_hTs[mt % 2]], writes=[b_pB[i2]], partial=True)
            S.op("act", lambda e: e.activation(out=sa[i2][:], in_=pA[i2][:], func=AF.Silu),
                 reads=[b_pA[i2]], writes=[b_sa[i2]])
            S.op("dve", lambda e: e.tensor_tensor(out=aT[:, j, :], in0=pB[i2][:], in1=sa[i2][:], op=ALU.mult),
                 reads=[b_pB[i2], b_sa[i2]], writes=[b_aT], partial=True)

        def b2_dn(mt, s):
            bi = 0 if mt < 8 else 1
            if s % 2 == 0:
                zh, bz = [pZ[:, 0:512], pZ[:, 512:1024]], b_pZh
            else:
                zh, bz = [pA[0][:], pB[0][:]], [b_pA[0], b_pB[0]]
            for n in range(2):
                for j in range(NJ):
                    S.op("pe", lambda e, n=n, j=j: e.matmul(
                        zh[n], lhsT=aT[:, j, 128 * s:128 * s + 128], rhs=wd[:, j, 512 * n:512 * n + 512],
                        start=(j == 0), stop=(j == NJ - 1)),
                        reads=[b_w, b_aT], writes=[bz[n]], partial=True)
            r0 = 512 * mt + 128 * s
            S.dma("sp", xres[:], x1_d[r0:r0 + 128, :], reads=[b_x1], writes=[b_xres])
            out_toks.append(backend(zh, bz, junk, ssm, b_ssm, ssm[:, 4:8], b_wkb, rowg2[:, bi, :], tmp, b_tmp,
                                    xres, b_xres, y_d[r0:r0 + 128, :], b_y))

        sl = b2_fe_sched(0)
        for j in sorted(sl):
            for f in sl[j]:
                f()
        for mt in range(NMT):
            nxt = b2_fe_sched(mt + 1) if mt + 1 < NMT else {}
            for j in range(NJ):
                b2_gu(mt, j)
                for f in nxt.get(j, []):
                    f()
            for s in range(4):
                b2_dn(mt, s)
        S.barrier()
        S.emit()
    top.close()
    return nc


_CACHE = {}


def _core_inputs(core, x_prompt, x_sample, c_prompt, c_sample, w, shared):
    b, q = core // 4, core % 4
    xa = np.concatenate([x_prompt[b], x_sample[core]], axis=0)
    xb = np.zeros((NRUN, 48, D), np.float32)
    msk = np.zeros((NRUN, 48), np.float32)
    for r, (sid, start, S) in enumerate(run_starts(q)):
        src = x_prompt[b] if sid == 0 else x_sample[core]
        lo, hi = start - 8, start + 40
        l2, h2 = max(lo, 0), min(hi, S)
        xb[r, l2 - lo:h2 - lo] = src[l2:h2]
        msk[r, l2 - lo:h2 - lo] = 1.0
    colv = np.zeros((128, 70), np.float32)

    def col(v):
        return np.ascontiguousarray(v.reshape(-1, 128).T)
    colv[:, 0:8] = col(c_prompt[b])
    colv[:, 8:16] = col(c_sample[core])
    colv[:, 16:24] = col(w["g_pre_mix"])
    colv[:, 24:32] = col(w["g_pre_ffn"])
    ba = w["b_ada"]
    colv[:, 32:40] = col(ba[0:D])
    colv[:, 40:48] = col(ba[D:2 * D])
    colv[:, 48:56] = col(ba[3 * D:4 * D])
    colv[:, 56:64] = col(ba[4 * D:5 * D])
    colv[:, 64:68] = col(w["pool_scale"])
    colv[:, 68] = 1.0 if q == 0 else 0.0
    colv[:, 69] = 1.0
    rowv = np.stack([w["g_post_mix"], w["g_post_ffn"], ba[2 * D:3 * D], ba[5 * D:6 * D]], axis=0)
    rowv = np.ascontiguousarray(np.broadcast_to(rowv[None], (128, 4, D)))
    E_p, T_p, E_s, T_s, CS = shared["four"][q]
    m = {
        "xa": np.ascontiguousarray(xa), "xb": xb.reshape(NRUN * 48, D),
        "msk": np.ascontiguousarray(np.broadcast_to(msk.reshape(1, -1), (128, NRUN * 48))),
        "icn": np.ascontiguousarray(np.broadcast_to(shared["icn"][q][None], (128, 4, NOWN))),
        "colv": colv, "rowv": rowv, "ident": shared["ident"],
        "Ep": E_p, "Tp": T_p, "Es": E_s, "Ts": T_s, "CS": CS,
        "w_ada": w["w_ada"], "w_in": w["w_in"], "w_fo": w["w_fo"], "w_pg": w["w_pg"], "w_po": w["w_po"],
        "w_out": w["w_out"], "w_gate": w["w_gate"], "w_up": w["w_up"], "w_down": w["w_down"],
    }
    return m


def kernel(x_prompt, x_sample, c_prompt, c_sample, w_ada, b_ada, g_pre_mix, w_in, w_fo, w_pg, pool_scale, w_po,
           w_out, g_post_mix, g_pre_ffn, w_gate, w_up, w_down, g_post_ffn):
    f = lambda a: np.ascontiguousarray(np.asarray(a, dtype=np.float32))
    x_prompt, x_sample, c_prompt, c_sample = f(x_prompt), f(x_sample), f(c_prompt), f(c_sample)
    w = {"w_ada": f(w_ada)[0], "b_ada": f(b_ada)[0], "g_pre_mix": f(g_pre_mix)[0], "w_in": f(w_in)[0],
         "w_fo": f(w_fo)[0], "w_pg": f(w_pg)[0], "pool_scale": f(pool_scale)[0], "w_po": f(w_po)[0],
         "w_out": f(w_out)[0], "g_post_mix": f(g_post_mix)[0], "g_pre_ffn": f(g_pre_ffn)[0],
         "w_gate": f(w_gate)[0], "w_up": f(w_up)[0], "w_down": f(w_down)[0], "g_post_ffn": f(g_post_ffn)[0]}
    if "shared" not in _CACHE:
        _CACHE["shared"] = {
            "four": [fourier_consts(q) for q in range(4)],
            "icn": [inv_counts(q) for q in range(4)],
            "ident": np.eye(128, dtype=np.float32).astype(BF),
        }
    shared = _CACHE["shared"]
    if "nc" not in _CACHE:
        _CACHE["nc"] = build_program()
    nc = _CACHE["nc"]
    in_maps = [_core_inputs(c, x_prompt, x_sample, c_prompt, c_sample, w, shared) for c in range(8)]
    res = run_bass_kernel_spmd(nc, in_maps, core_ids=list(range(8)))
    y_prompt = np.empty((2, SP, D), np.float32)
    y_sample = np.empty((8, SS, D), np.float32)
    for c in range(8):
        b, q = c // 4, c % 4
        y = np.asarray(res.results[c]["y"], dtype=np.float32)
        yp = y[:4096].reshape(128, 32, D)
        y_prompt[b].reshape(128, 128, D)[:, 32 * q:32 * q + 32, :] = yp
        y_sample[c] = y[4096:]
    return (y_prompt, y_sample)
```

```python
from contextlib import ExitStack
import numpy as np
import ml_dtypes
import concourse.bass as bass
import concourse.mybir as mybir
from concourse.bass_utils import run_bass_kernel_spmd

F32 = mybir.dt.float32
BF16 = mybir.dt.bfloat16
ALU = mybir.AluOpType
AF = mybir.ActivationFunctionType
BF = ml_dtypes.bfloat16

D = 1024
SP = 16384
SS = 2048
DFF = 2816
NJ = 22
NRUN = 192
NOWN = 6144
NMT = 12
EPS = 1e-6
WINS = (2, 4, 8, 16)


class Buf:
    __slots__ = ("w", "r", "prev")

    def __init__(self):
        self.w = []
        self.r = []
        self.prev = []


class Sched:
    ENG = ("pe", "act", "dve", "pool", "sp")

    def __init__(self, nc):
        self.nc = nc
        self.q = {e: [] for e in self.ENG}
        self.cnt = {e: 0 for e in self.ENG}
        self.sem = {e: nc.alloc_semaphore(name="ms_" + e) for e in self.ENG}
        self.seen = {e: {} for e in self.ENG}
        self.dma_sems = [nc.alloc_semaphore(name="dq%d" % i) for i in range(32)]
        self.dma_cnt = [0] * len(self.dma_sems)
        self.dma_rr = 0

    @staticmethod
    def _merge(toks):
        d = {}
        for k, v in toks:
            if d.get(k, 0) < v:
                d[k] = v
        return list(d.items())

    def _deps(self, reads, writes, partial):
        deps = []
        for b in reads:
            deps += b.w
        for b in writes:
            if partial and not b.r:
                deps += b.prev
            else:
                b.prev = self._merge(b.w + b.r)
                b.w = []
                b.r = []
                deps += b.prev
        return self._merge(deps)

    def _filter(self, eng, deps):
        out = []
        seen = self.seen[eng]
        for k, v in deps:
            if eng == "pe" and k == "pe":
                continue
            if seen.get(k, 0) >= v:
                continue
            seen[k] = v
            out.append((k, v))
        return out

    def op(self, eng, fn, reads=(), writes=(), partial=False):
        deps = self._filter(eng, self._deps(reads, writes, partial))
        self.cnt[eng] += 1
        tok = (eng, self.cnt[eng])
        for b in reads:
            b.r = self._merge(b.r + [tok])
        for b in writes:
            b.w = self._merge(b.w + [tok])
        self.q[eng].append((fn, deps, None))
        return tok

    def dma(self, eng, out, in_, reads=(), writes=(), partial=False):
        i = self.dma_rr
        self.dma_rr = (self.dma_rr + 1) % len(self.dma_sems)
        deps = self._deps(reads, writes, partial)
        if self.dma_cnt[i] > 0:
            deps = self._merge(deps + [(("dma", i), self.dma_cnt[i])])
        deps = self._filter(eng, deps)
        self.dma_cnt[i] += 16
        tok = (("dma", i), self.dma_cnt[i])
        for b in reads:
            b.r = self._merge(b.r + [tok])
        for b in writes:
            b.w = self._merge(b.w + [tok])
        self.q[eng].append((lambda e: e.dma_start(out=out, in_=in_), deps, i))
        return tok

    def barrier(self):
        toks = [(e, self.cnt[e]) for e in self.ENG if self.cnt[e] > 0]
        toks += [(("dma", i), c) for i, c in enumerate(self.dma_cnt) if c > 0]
        for e in self.ENG:
            deps = self._filter(e, list(toks))
            self.q[e].append((None, deps, None))

    def _semof(self, k):
        if isinstance(k, tuple):
            return self.dma_sems[k[1]]
        return self.sem[k]

    def emit(self):
        def run(eng_name):
            def body(e):
                for fn, deps, dsem in self.q[eng_name]:
                    for k, v in deps:
                        e.wait_ge(self._semof(k), v)
                    if fn is None:
                        continue
                    ins = fn(e)
                    if dsem is None:
                        ins.then_inc(self.sem[eng_name], 1)
                    else:
                        ins.then_inc(self.dma_sems[dsem], 16)
            return body

        with self.nc.Block() as block:
            block.tensor(run("pe"))
            block.scalar(run("act"))
            block.vector(run("dve"))
            block.gpsimd(run("pool"))
            block.sync(run("sp"))
        self.q = {e: [] for e in self.ENG}


def fourier_consts(q):
    s2 = np.arange(128)
    k2 = 32 * q + np.arange(32)
    ang = 2 * np.pi * ((np.outer(s2, k2)) % 128) / 128.0
    E_p = np.empty((128, 2, 32))
    E_p[:, 0, :] = np.cos(ang)
    E_p[:, 1, :] = -np.sin(ang)
    E_p = E_p.reshape(128, 64)
    s1 = np.arange(128)
    k1 = np.arange(128)
    T_p = np.zeros((32, 128, 2, 256))
    for g in range(32):
        k = 128 * k1 + k2[g]
        th = 2 * np.pi * ((np.outer(s1, k)) % SP) / float(SP)
        Tr, Ti = np.cos(th), -np.sin(th)
        T_p[g, :, 0, :128] = Tr
        T_p[g, :, 0, 128:] = Ti
        T_p[g, :, 1, :128] = -Ti
        T_p[g, :, 1, 128:] = Tr
    ang = 2 * np.pi * ((np.outer(s2, np.arange(128))) % 128) / 128.0
    E_s = np.empty((128, 128, 2))
    E_s[..., 0] = np.cos(ang)
    E_s[..., 1] = -np.sin(ang)
    E_s = E_s.reshape(128, 256)
    T_s = np.zeros((16, 128, 2, 256))
    for B in range(16):
        for s1v in range(16):
            for ks in range(8):
                r = s1v * 8 + ks
                k1v = np.arange(16)
                k = 128 * k1v + 8 * B + ks
                th = 2 * np.pi * ((k * s1v) % SS) / float(SS)
                Tr, Ti = np.cos(th), -np.sin(th)
                col = k1v * 8 + ks
                T_s[B, r, 0, col] = Tr
                T_s[B, r, 0, 128 + col] = Ti
                T_s[B, r, 1, col] = -Ti
                T_s[B, r, 1, 128 + col] = Tr
    c = np.arange(128)
    angc = 2 * np.pi * ((np.outer(c, c)) % 128) / 128.0
    CS = np.stack([np.cos(angc), np.sin(angc)], axis=1)
    return (E_p.astype(BF), T_p.astype(BF), E_s.astype(BF), T_s.astype(BF), CS.astype(BF))


def run_starts(q):
    st = [(0, 128 * k1 + 32 * q, SP) for k1 in range(128)]
    st += [(1, 32 * r, SS) for r in range(64)]
    return st


def inv_counts(q):
    icn = np.zeros((4, NOWN), np.float32)
    for r, (_, start, S) in enumerate(run_starts(q)):
        t = start + np.arange(32)
        for g, w in enumerate(WINS):
            left = w // 2
            right = w - 1 - left
            cnt = np.minimum(t + right, S - 1) - np.maximum(t - left, 0) + 1
            icn[g, 32 * r:32 * r + 32] = 1.0 / cnt.astype(np.float32)
    return icn


def build_program():
    nc = bass.Bass("TRN2", target_bir_lowering=False)

    def din(name, shape, dt=F32):
        return nc.dram_tensor(name, list(shape), dt, kind="ExternalInput").ap()

    xa = din("xa", [SP + SS, D])
    xb = din("xb", [NRUN * 48, D])
    msk_d = din("msk", [128, NRUN * 48])
    icn_d = din("icn", [128, 4, NOWN])
    colv_d = din("colv", [128, 70])
    rowv_d = din("rowv", [128, 4, D])
    ident_d = din("ident", [128, 128], BF16)
    Ep_d = din("Ep", [128, 64], BF16)
    Tp_d = din("Tp", [32, 128, 2, 256], BF16)
    Es_d = din("Es", [128, 256], BF16)
    Ts_d = din("Ts", [16, 128, 2, 256], BF16)
    CS_d = din("CS", [128, 2, 128], BF16)
    w_ada = din("w_ada", [D, 6 * D])
    w_in = din("w_in", [D, 3072])
    w_fo = din("w_fo", [512, D])
    w_pg = din("w_pg", [4, 128, 128])
    w_po = din("w_po", [512, D])
    w_out = din("w_out", [D, D])
    w_gate = din("w_gate", [D, DFF])
    w_up = din("w_up", [D, DFF])
    w_down = din("w_down", [DFF, D])
    y_d = nc.dram_tensor("y", [NOWN, D], F32, kind="ExternalOutput").ap()
    zf_d = nc.dram_tensor("zf_scr", [4, 128, NOWN], BF16).ap()
    x1_d = nc.dram_tensor("x1_scr", [NOWN, D], F32).ap()
    wg_bf = nc.dram_tensor("wg_bf", [D, DFF], BF16).ap()
    wu_bf = nc.dram_tensor("wu_bf", [D, DFF], BF16).ap()
    wd_bf = nc.dram_tensor("wd_bf", [DFF, D], BF16).ap()
    win_bf = nc.dram_tensor("win_bf", [D, 2560], BF16).ap()
    wfo_bf = nc.dram_tensor("wfo_bf", [512, D], BF16).ap()
    wpo_bf = nc.dram_tensor("wpo_bf", [512, D], BF16).ap()
    wout_bf = nc.dram_tensor("wout_bf", [D, D], BF16).ap()
    wpg_bf = nc.dram_tensor("wpg_bf", [4, 128, 128], BF16).ap()

    S = Sched(nc)
    top = ExitStack()

    def sb(es, name, shape, dt):
        return es.enter_context(nc.sbuf_tensor("s_" + name, list(shape), dt))

    def ps(es, name, shape, dt):
        return es.enter_context(nc.psum_tensor("p_" + name, list(shape), dt))

    ident = sb(top, "ident", [128, 128], BF16)
    colv = sb(top, "colv", [128, 70], F32)
    acol = sb(top, "acol", [128, 2, 4, 8], F32)
    rowg2 = sb(top, "rowg2", [128, 2, D], F32)
    mid = ExitStack()
    rowg1 = sb(mid, "rowg1", [128, 2, D], F32)
    rowg_l = [rowg1, rowg2]
    b_ident, b_colv, b_acol, b_rowg = Buf(), Buf(), Buf(), Buf()
    b_zf, b_x1 = Buf(), Buf()
    b_wscr = Buf()
    for dst_, src_ in ((win_bf, w_in[:, 512:3072]), (wfo_bf, w_fo), (wpo_bf, w_po), (wout_bf, w_out), (wpg_bf, w_pg),
                       (wg_bf, w_gate)):
        S.dma("pool", dst_, src_, writes=[b_wscr], partial=True)
    S.dma("sp", ident[:], ident_d, writes=[b_ident])
    S.dma("sp", colv[:], colv_d, writes=[b_colv])
    out_toks = []

    def act_rstd(dst, src, b_dst, b_src):
        S.op("act", lambda e: e.activation(out=dst, in_=src, func=AF.Ln, scale=1.0 / D, bias=EPS),
             reads=[b_src], writes=[b_dst])
        S.op("act", lambda e: e.activation(out=dst, in_=dst, func=AF.Exp, scale=-0.5),
             reads=[b_dst], writes=[b_dst])

    with ExitStack() as es:
        stg = [sb(es, "p0_stg%d" % i, [128, 8, 512], F32) for i in range(2)]
        wbf = [sb(es, "p0_wbf%d" % i, [128, 8, 512], BF16) for i in range(2)]
        b_stg = [Buf(), Buf()]
        b_wbf = [Buf(), Buf()]
        scb = sb(es, "p0_sc", [128, 8, 2], BF16)
        screp = sb(es, "p0_screp", [128, 2, 8, 128], BF16)
        rowv = sb(es, "p0_rowv", [128, 4, D], F32)
        adac = sb(es, "p0_adac", [128, 4, 8, 2], F32)
        b_scb, b_screp, b_rowv, b_adac = Buf(), Buf(), Buf(), Buf()
        pc = ps(es, "p0_pc", [128, 4, 2], F32)
        pr = [ps(es, "p0_pr%d" % i, [128, 512], F32) for i in range(2)]
        b_pc = Buf()
        b_pr = [Buf(), Buf()]
        S.dma("sp", rowv[:], rowv_d, writes=[b_rowv])
        for b in range(2):
            S.op("act", lambda e, b=b: e.activation(out=scb[:, :, b], in_=colv[:, 8 * b:8 * b + 8], func=AF.Silu),
                 reads=[b_colv], writes=[b_scb], partial=True)
        for b in range(2):
            for k in range(8):
                S.op("dve", lambda e, b=b, k=k: e.tensor_copy(
                    out=screp[:, b, k, :], in_=scb[:, k, b:b + 1].to_broadcast([128, 128])),
                    reads=[b_scb], writes=[b_screp], partial=True)
        for pc_i in range(12):
            i = pc_i % 2
            S.dma("sp", stg[i][:], w_ada[:, 512 * pc_i:512 * pc_i + 512].rearrange("(k p) c -> p k c", p=128),
                  writes=[b_stg[i]])
            S.op("dve", lambda e, i=i: e.tensor_copy(out=wbf[i][:], in_=stg[i][:]),
                 reads=[b_stg[i]], writes=[b_wbf[i]])
            vec = pc_i // 2
            half = pc_i % 2
            if vec in (2, 5):
                gi = 0 if vec == 2 else 1
                for b in range(2):
                    for k in range(8):
                        S.op("pe", lambda e, b=b, k=k, i=i: e.matmul(
                            pr[b][:], lhsT=screp[:, b, k, :], rhs=wbf[i][:, k, :], start=(k == 0), stop=(k == 7)),
                            reads=[b_screp, b_wbf[i]], writes=[b_pr[b]], partial=True)
                    dst = rowg_l[gi][:, b, 512 * half:512 * half + 512]
                    S.op("dve", lambda e, b=b, dst=dst, gi=gi, half=half: e.tensor_tensor(
                        out=dst, in0=pr[b][:], in1=rowv[:, 2 + gi, 512 * half:512 * half + 512], op=ALU.add),
                        reads=[b_pr[b], b_rowv], writes=[b_rowg], partial=True)
                    S.op("dve", lambda e, dst=dst, gi=gi, half=half: e.tensor_tensor(
                        out=dst, in0=dst, in1=rowv[:, gi, 512 * half:512 * half + 512], op=ALU.mult),
                        reads=[b_rowv, b_rowg], writes=[b_rowg], partial=True)
            else:
                vi = {0: 0, 1: 1, 3: 2, 4: 3}[vec]
                for c in range(4):
                    for k in range(8):
                        S.op("pe", lambda e, c=c, k=k, i=i: e.matmul(
                            pc[:, c, :], lhsT=wbf[i][:, k, 128 * c:128 * c + 128], rhs=scb[:, k, :],
                            start=(k == 0), stop=(k == 7)),
                            reads=[b_scb, b_wbf[i]], writes=[b_pc], partial=True)
                S.op("dve", lambda e, vi=vi, half=half: e.tensor_copy(
                    out=adac[:, vi, 4 * half:4 * half + 4, :], in_=pc[:]),
                    reads=[b_pc], writes=[b_adac], partial=True)
        for b in range(2):
            for si, (gcol, shv, scv) in enumerate(((16, 0, 1), (24, 2, 3))):
                bsh = 32 + 16 * si
                bsc = 40 + 16 * si
                a_dst = acol[:, b, 2 * si, :]
                b_dst_ = acol[:, b, 2 * si + 1, :]
                S.op("dve", lambda e, a_dst=a_dst, scv=scv, b=b, bsc=bsc: e.tensor_tensor(
                    out=a_dst, in0=adac[:, scv, :, b], in1=colv[:, bsc:bsc + 8], op=ALU.add),
                    reads=[b_adac, b_colv], writes=[b_acol], partial=True)
                S.op("dve", lambda e, a_dst=a_dst, gcol=gcol: e.scalar_tensor_tensor(
                    out=a_dst, in0=a_dst, scalar=1.0, in1=colv[:, gcol:gcol + 8], op0=ALU.add, op1=ALU.mult),
                    reads=[b_acol, b_colv], writes=[b_acol], partial=True)
                S.op("dve", lambda e, b_dst_=b_dst_, shv=shv, b=b, bsh=bsh: e.tensor_tensor(
                    out=b_dst_, in0=adac[:, shv, :, b], in1=colv[:, bsh:bsh + 8], op=ALU.add),
                    reads=[b_adac, b_colv], writes=[b_acol], partial=True)
        S.barrier()
        S.emit()

    MAGIC = 0x5F3759DF
    I32 = mybir.dt.int32

    def pipeline(stages, n):
        ns = len(stages)
        for step in range(n + ns - 1):
            for si in reversed(range(ns)):
                t = step - si
                if 0 <= t < n:
                    stages[si](t)

    def dve_rstd(wk, b_wk, srcs, b_src, out, b_out):
        a, yv, tv = wk[:, 0:1], wk[:, 1:2], wk[:, 2:3]
        if len(srcs) == 2:
            S.op("dve", lambda e: e.tensor_tensor(out=a, in0=srcs[0], in1=srcs[1], op=ALU.add),
                 reads=[b_src], writes=[b_wk])
            S.op("dve", lambda e: e.tensor_scalar(out=a, in0=a, scalar1=1.0 / D, scalar2=EPS, op0=ALU.mult, op1=ALU.add),
                 reads=[b_wk], writes=[b_wk])
        else:
            S.op("dve", lambda e: e.tensor_scalar(out=a, in0=srcs[0], scalar1=1.0 / D, scalar2=EPS, op0=ALU.mult,
                                                  op1=ALU.add), reads=[b_src], writes=[b_wk])
        S.op("dve", lambda e: e.tensor_single_scalar(out=yv.bitcast(I32), in_=a.bitcast(I32), scalar=1,
                                                     op=ALU.arith_shift_right), reads=[b_wk], writes=[b_wk])
        S.op("dve", lambda e: e.tensor_scalar(out=yv.bitcast(I32), in0=yv.bitcast(I32), scalar1=-1, scalar2=MAGIC,
                                              op0=ALU.mult, op1=ALU.add), reads=[b_wk], writes=[b_wk])
        for it in range(2):
            S.op("dve", lambda e: e.scalar_tensor_tensor(out=tv, in0=yv, scalar=a, in1=yv, op0=ALU.mult, op1=ALU.mult),
                 reads=[b_wk], writes=[b_wk])
            S.op("dve", lambda e: e.tensor_scalar(out=tv, in0=tv, scalar1=-0.5, scalar2=1.5, op0=ALU.mult, op1=ALU.add),
                 reads=[b_wk], writes=[b_wk])
            if it == 0:
                S.op("dve", lambda e: e.tensor_tensor(out=yv, in0=yv, in1=tv, op=ALU.mult), reads=[b_wk], writes=[b_wk])
            else:
                S.op("dve", lambda e: e.tensor_tensor(out=out, in0=yv, in1=tv, op=ALU.mult),
                     reads=[b_wk], writes=[b_out])

    def transposes(xn, b_xn, pT, b_pT):
        for k in range(8):
            S.op("pe", lambda e, k=k: e.transpose(out=pT[:, 128 * k:128 * k + 128], in_=xn[:, 128 * k:128 * k + 128],
                                                  identity=ident[:]),
                 reads=[b_xn, b_ident], writes=[b_pT], partial=True)

    def evacs(pT, b_pT, hT_dst, b_hT, bi, ni):
        for k in range(8):
            S.op("dve", lambda e, k=k: e.tensor_scalar(
                out=hT_dst(k), in0=pT[:, 128 * k:128 * k + 128],
                scalar1=acol[:, bi, 2 * ni, k:k + 1], scalar2=acol[:, bi, 2 * ni + 1, k:k + 1],
                op0=ALU.mult, op1=ALU.add),
                reads=[b_pT, b_acol], writes=[b_hT], partial=True)

    with ExitStack() as es:
        wf = sb(es, "a_wf", [128, 8, 512], BF16)
        wfa = [sb(es, "a_wfa%d" % i, [128, 8, 512], BF16) for i in range(2)]
        fixv = sb(es, "a_fixv", [128, 4, 2], F32)
        b_wf, b_fixv = Buf(), Buf()
        CSs = sb(es, "a_cs", [128, 2, 128], BF16)
        b_CS = Buf()
        S.dma("sp", CSs[:], CS_d, writes=[b_CS])
        with ExitStack() as es2:
            stg = sb(es2, "a_stg", [128, 8, 512], F32)
            b_st = Buf()
            S.dma("sp", stg[:], w_in[:, 0:512].rearrange("(k p) c -> p k c", p=128), writes=[b_st])
            S.op("dve", lambda e: e.tensor_copy(out=wf[:], in_=stg[:]), reads=[b_st], writes=[b_wf])
            bcol = sb(es2, "a_bcol", [128, 2, 8], BF16)
            b_bcol = Buf()
            S.op("dve", lambda e: e.tensor_copy(out=bcol[:], in_=acol[:, :, 1, :]), reads=[b_acol], writes=[b_bcol])
            pbw = ps(es2, "a_pbw", [128, 4, 2], F32)
            b_pbw = Buf()
            for b in range(2):
                for k in range(8):
                    S.op("dve", lambda e, b=b, k=k: e.tensor_scalar(
                        out=wfa[b][:, k, :], in0=stg[:, k, :], scalar1=acol[:, b, 0, k:k + 1], scalar2=None,
                        op0=ALU.mult), reads=[b_st, b_acol], writes=[b_wf], partial=True)
                for j in range(4):
                    for k in range(8):
                        S.op("pe", lambda e, b=b, j=j, k=k: e.matmul(
                            pbw[:, j, b:b + 1], lhsT=wf[:, k, 128 * j:128 * j + 128], rhs=bcol[:, b, k:k + 1],
                            start=(k == 0), stop=(k == 7)), reads=[b_wf, b_bcol], writes=[b_pbw], partial=True)
                S.op("dve", lambda e, b=b: e.tensor_scalar(
                    out=fixv[:, :, b], in0=pbw[:, :, b], scalar1=colv[:, 68 + b:69 + b],
                    scalar2=float(SP if b == 0 else SS), op0=ALU.mult, op1=ALU.mult),
                    reads=[b_pbw, b_colv], writes=[b_fixv], partial=True)
            S.barrier()
            S.emit()
        NX = 4
        xts = [sb(es, "a_xt%d" % i, [128, D], F32) for i in range(NX)]
        b_xts = [Buf() for _ in range(NX)]
        junk = sb(es, "a_junk", [128, D], BF16)
        ND = 3
        xns = [sb(es, "a_xn%d" % i, [128, D], BF16) for i in range(ND)]
        b_xns = [Buf() for _ in range(ND)]
        hTs = [sb(es, "a_hT%d" % i, [128, 8, 128], BF16) for i in range(ND)]
        b_hTs = [Buf() for _ in range(ND)]
        us = [sb(es, "a_u%d" % i, [128, 512], BF16) for i in range(ND)]
        b_us = [Buf() for _ in range(ND)]
        ssr = [sb(es, "a_ss%d" % i, [128, 2], F32) for i in range(ND)]
        b_ssr = [Buf() for _ in range(ND)]
        b_rsr = [Buf() for _ in range(ND)]
        HT = sb(es, "a_HT", [128, 4 * 2 * 32 * 128], BF16)
        b_HT = Buf()
        Eps = sb(es, "a_Ep", [128, 64], BF16)
        Ess = sb(es, "a_Es", [128, 256], BF16)
        b_E = Buf()
        S.dma("sp", Eps[:], Ep_d, writes=[b_E], partial=True)
        S.dma("sp", Ess[:], Es_d, writes=[b_E], partial=True)
        Tg = [sb(es, "a_T%d" % i, [128, 2, 256], BF16) for i in range(3)]
        b_Tg = [Buf() for _ in range(3)]
        Hg = [sb(es, "a_Hg%d" % i, [128, 2, 512], BF16) for i in range(2)]
        b_Hg = [Buf(), Buf()]
        Pg = [sb(es, "a_Pg%d" % i, [128, 4, 256], BF16) for i in range(2)]
        b_Pg = [Buf(), Buf()]
        Zf = sb(es, "a_Zf", [128, 4, NOWN], BF16)
        b_Zf = Buf()
        pTs = [ps(es, "a_pT%d" % i, [128, D], BF16) for i in range(2)]
        b_pTs = [Buf(), Buf()]
        pUs = [ps(es, "a_pU%d" % i, [128, 512], F32) for i in range(2)]
        b_pUs = [Buf(), Buf()]
        p1s = [ps(es, "a_p1%d" % i, [128, 1024], F32) for i in range(2)]
        b_p1s = [Buf(), Buf()]

        for seq in range(2):
            if seq == 0:
                row0, Sq, N1, NE, NG = 0, SP, 128, 64, 32
                E_sb = Eps
                HTv = HT[:].rearrange("p (s c) -> p s c", s=128)
            else:
                row0, Sq, N1, NE, NG = SP, SS, 16, 256, 16
                E_sb = Ess
                HTv = HT[:, 0:4 * 2 * 16 * 128].rearrange("p (j r g s k) -> p j r g s k", j=4, r=2, g=16, s=16, k=8)
            xseq = xa[row0:row0 + Sq, :].rearrange("(s2 s1) d -> s1 s2 d", s1=N1)

            def a_s0(t):
                S.dma("sp", xts[t % NX][:], xseq[t], writes=[b_xts[t % NX]])

            def a_s1(t):
                xt, bx = xts[t % NX], b_xts[t % NX]
                ss, rs = ssr[t % ND][:, 0:1], ssr[t % ND][:, 1:2]
                S.op("act", lambda e: e.activation(out=junk[:], in_=xt[:], func=AF.Square, accum_out=ss),
                     reads=[bx], writes=[b_ssr[t % ND]])
                act_rstd(rs, ss, b_rsr[t % ND], b_ssr[t % ND])

            def a_s2(t):
                xt, bx = xts[t % NX], b_xts[t % NX]
                rs = ssr[t % ND][:, 1:2]
                xn = xns[t % ND]
                S.op("pool", lambda e: e.tensor_tensor(out=xn[:], in0=xt[:], in1=rs.to_broadcast([128, D]), op=ALU.mult),
                     reads=[bx, b_rsr[t % ND]], writes=[b_xns[t % ND]])

            def a_s3(t):
                transposes(xns[t % ND], b_xns[t % ND], pTs[t % 2], b_pTs[t % 2])

            def a_s4(t, seq=seq):
                hT, pT = hTs[t % ND], pTs[t % 2]
                S.op("dve", lambda e: e.tensor_copy(out=hT[:].rearrange("p k t -> p (k t)"), in_=pT[:]),
                     reads=[b_pTs[t % 2]], writes=[b_hTs[t % ND]])

            def a_s5(t, seq=seq):
                hT, pU = hTs[t % ND], pUs[t % 2]
                for k in range(8):
                    S.op("pe", lambda e, k=k: e.matmul(pU[:], lhsT=hT[:, k, :], rhs=wfa[seq][:, k, :],
                                                       start=(k == 0), stop=(k == 7)),
                         reads=[b_hTs[t % ND], b_wf], writes=[b_pUs[t % 2]], partial=True)

            def a_s6(t):
                u, pU = us[t % ND], pUs[t % 2]
                S.op("act", lambda e: e.activation(out=u[:], in_=pU[:], func=AF.Copy),
                     reads=[b_pUs[t % 2]], writes=[b_us[t % ND]])

            def a_s7(t, E_sb=E_sb, NE=NE):
                u, p1 = us[t % ND], p1s[t % 2]
                for j in range(4):
                    S.op("pe", lambda e, j=j: e.matmul(p1[:, NE * j:NE * j + NE], lhsT=u[:, 128 * j:128 * j + 128],
                                                       rhs=E_sb[:], start=True, stop=True),
                         reads=[b_us[t % ND], b_E], writes=[b_p1s[t % 2]], partial=True)

            def a_s8(t, seq=seq, NE=NE, HTv=HTv):
                p1 = p1s[t % 2]
                p1v = p1[:, 0:4 * NE].rearrange("p (j e) -> p j e", j=4)
                for r in range(2):
                    if seq == 0:
                        if r == 1:
                            continue
                        dst = HTv[:, t, :]
                        src = p1[:, 0:256]
                    else:
                        dst = HTv[:, :, r, :, t, :]
                        src = p1v.rearrange("p j (g k r) -> p j r g k", r=2, k=8)[:, :, r, :, :]
                    S.op("dve", lambda e, dst=dst, src=src: e.tensor_copy(out=dst, in_=src),
                         reads=[b_p1s[t % 2]], writes=[b_HT], partial=True)

            pipeline([a_s0, a_s1, a_s2, a_s3, a_s4, a_s5, a_s6, a_s7, a_s8], N1)
            if seq == 1:
                S.dma("pool", wu_bf, w_up, writes=[b_wscr], partial=True)
                S.dma("pool", wd_bf, w_down, writes=[b_wscr], partial=True)

            scale = 1.0 / float(np.sqrt(Sq * 128.0))
            T_d = Tp_d if seq == 0 else Ts_d

            def g_s0(g, T_d=T_d):
                S.dma("sp", Tg[g % 3][:], T_d[g], writes=[b_Tg[g % 3]])

            def g_s1(g, seq=seq, HTv=HTv):
                pT = pTs[g % 2]
                for r in range(2):
                    for j in range(4):
                        if seq == 0:
                            src = HTv[:, :, j * 64 + r * 32 + g]
                        else:
                            src = HTv[:, j, r, g, :, :].rearrange("p s k -> p (s k)")
                        c0 = 512 * r + 128 * j
                        S.op("pe", lambda e, src=src, c0=c0: e.transpose(out=pT[:, c0:c0 + 128], in_=src,
                                                                         identity=ident[:]),
                             reads=[b_HT, b_ident], writes=[b_pTs[g % 2]], partial=True)

            def g_s2(g):
                pT, H = pTs[g % 2], Hg[g % 2]
                S.op("act", lambda e: e.activation(out=H[:].rearrange("p r c -> p (r c)"), in_=pT[:], func=AF.Copy),
                     reads=[b_pTs[g % 2]], writes=[b_Hg[g % 2]])

            def g_s3(g):
                H, T, p1 = Hg[g % 2], Tg[g % 3], p1s[g % 2]
                for j in range(4):
                    for r in range(2):
                        S.op("pe", lambda e, j=j, r=r: e.matmul(
                            p1[:, 256 * j:256 * j + 256], lhsT=H[:, r, 128 * j:128 * j + 128],
                            rhs=T[:, r, :], start=(r == 0), stop=(r == 1)),
                            reads=[b_Hg[g % 2], b_Tg[g % 3]], writes=[b_p1s[g % 2]], partial=True)

            def g_s4(g, seq=seq):
                P, p1 = Pg[g % 2], p1s[g % 2]
                S.op("dve", lambda e: e.tensor_copy(out=P[:].rearrange("p j c -> p (j c)"), in_=p1[:]),
                     reads=[b_p1s[g % 2]], writes=[b_Pg[g % 2]])
                if g == 0:
                    S.op("dve", lambda e: e.tensor_tensor(
                        out=P[:, :, 0:1], in0=p1[:].rearrange("p (j c) -> p j c", j=4)[:, :, 0:1],
                        in1=fixv[:, :, seq:seq + 1], op=ALU.add),
                        reads=[b_p1s[g % 2], b_fixv, b_Pg[g % 2]], writes=[b_Pg[g % 2]], partial=True)

            def g_s5(g):
                P, pU = Pg[g % 2], pUs[g % 2]
                for j in range(4):
                    for r in range(2):
                        S.op("pe", lambda e, j=j, r=r: e.matmul(
                            pU[:, 128 * j:128 * j + 128], lhsT=CSs[:, r, :], rhs=P[:, j, 128 * r:128 * r + 128],
                            start=(r == 0), stop=(r == 1)),
                            reads=[b_Pg[g % 2], b_CS], writes=[b_pUs[g % 2]], partial=True)

            def g_s6(g, seq=seq, scale=scale):
                pU = pUs[g % 2]
                if seq == 0:
                    dst = Zf[:, :, 0:4096].rearrange("p j (k g) -> p j g k", g=32)[:, :, g, :]
                    src = pU[:].rearrange("p (j k) -> p j k", j=4)
                else:
                    dst = Zf[:, :, 4096:NOWN].rearrange("p j (k b s) -> p j b k s", b=16, s=8)[:, :, g, :, :]
                    src = pU[:].rearrange("p (j k s) -> p j k s", j=4, s=8)
                S.op("dve", lambda e: e.tensor_scalar(out=dst, in0=src, scalar1=scale, scalar2=None, op0=ALU.mult),
                     reads=[b_pUs[g % 2]], writes=[b_Zf], partial=True)

            pipeline([g_s0, g_s1, g_s2, g_s3, g_s4, g_s5, g_s6], NG)
        for j in range(4):
            S.dma("sp", zf_d[j], Zf[:, j, :], reads=[b_Zf], writes=[b_zf], partial=True)
        S.barrier()
        S.emit()

    def load_weight(dst, src, K, cols, stgs, b_stgs, b_dst, ctr):
        step = 4096 // K
        for c0 in range(0, cols, step):
            cw = min(step, cols - c0)
            i = ctr[0] % 2
            ctr[0] += 1
            view = stgs[i][:, 0:K * cw].rearrange("p (k c) -> p k c", k=K)
            S.dma("sp", view, src[:, c0:c0 + cw].rearrange("(k p) c -> p k c", p=128), writes=[b_stgs[i]])
            S.op("dve" if i == 0 else "pool", lambda e, view=view, c0=c0, cw=cw: e.tensor_copy(
                out=dst[:, :, c0:c0 + cw], in_=view), reads=[b_stgs[i]], writes=[b_dst], partial=True)

    def backend(zh, b_zh, junk, ssm, b_ssm, wk, b_wk, gg, tmp, b_tmp, xres, b_xres, dst_rows, b_dst):
        for n in range(2):
            S.op("act", lambda e, n=n: e.activation(out=junk[:, 0:512], in_=zh[n], func=AF.Square,
                                                    accum_out=ssm[:, n:n + 1]),
                 reads=[b_zh[n]], writes=[b_ssm], partial=(n == 1))
        dve_rstd(wk, b_wk, [ssm[:, 0:1], ssm[:, 1:2]], b_ssm, ssm[:, 3:4], b_ssm)
        for n in range(2):
            S.op("dve", lambda e, n=n: e.scalar_tensor_tensor(
                out=tmp[:, 512 * n:512 * n + 512], in0=zh[n], scalar=ssm[:, 3:4],
                in1=gg[:, 512 * n:512 * n + 512], op0=ALU.mult, op1=ALU.mult),
                reads=[b_zh[n], b_ssm, b_rowg], writes=[b_tmp], partial=(n == 1))
        S.op("pool", lambda e: e.tensor_tensor(out=tmp[:], in0=tmp[:], in1=xres[:], op=ALU.add),
             reads=[b_tmp, b_xres], writes=[b_tmp])
        return S.dma("sp", dst_rows, tmp[:], reads=[b_tmp], writes=[b_dst], partial=True)

    with ExitStack() as es:
        wi = sb(es, "b1_wi", [128, 8, 2560], BF16)
        wfo = sb(es, "b1_wfo", [128, 4, D], BF16)
        wpo = sb(es, "b1_wpo", [128, 4, D], BF16)
        wou = sb(es, "b1_wou", [128, 8, D], BF16)
        wpg = sb(es, "b1_wpg", [128, 4, 128], BF16)
        b_w = Buf()
        for c0 in range(0, 2560, 640):
            S.dma("sp", wi[:, :, c0:c0 + 640], win_bf[:, c0:c0 + 640].rearrange("(k p) c -> p k c", p=128),
                  reads=[b_wscr], writes=[b_w], partial=True)
        S.dma("sp", wfo[:], wfo_bf.rearrange("(k p) c -> p k c", p=128), reads=[b_wscr], writes=[b_w], partial=True)
        S.dma("sp", wpo[:], wpo_bf.rearrange("(k p) c -> p k c", p=128), reads=[b_wscr], writes=[b_w], partial=True)
        for c0 in range(0, D, 512):
            S.dma("sp", wou[:, :, c0:c0 + 512], wout_bf[:, c0:c0 + 512].rearrange("(k p) c -> p k c", p=128),
                  reads=[b_wscr], writes=[b_w], partial=True)
        S.dma("sp", wpg[:], wpg_bf.rearrange("g c d -> c g d"), reads=[b_wscr], writes=[b_w], partial=True)
        S.barrier()
        S.emit()
        NX = 2
        xts = [sb(es, "b1_xt%d" % i, [128, D], F32) for i in range(NX)]
        b_xts = [Buf() for _ in range(NX)]
        junk = sb(es, "b1_junk", [128, D], BF16)
        NXN = 3
        xns = [sb(es, "b1_xn%d" % i, [128, D], BF16) for i in range(NXN)]
        b_xns = [Buf() for _ in range(NXN)]
        ssr = [sb(es, "b1_ss%d" % i, [128, 8], F32) for i in range(NXN)]
        b_ssr = [Buf() for _ in range(NXN)]
        b_rsr = [Buf() for _ in range(NXN)]
        b_wkf = [Buf() for _ in range(NXN)]
        hTs = [sb(es, "b1_hT%d" % i, [128, 8, 768], BF16) for i in range(2)]
        b_hTs = [Buf(), Buf()]
        up = sb(es, "b1_up", [128, 4, 768], F32)
        b_up = Buf()
        A2 = sb(es, "b1_A2", [128, 16, 48], F32)
        A4 = sb(es, "b1_A4", [128, 16, 48], F32)
        A8 = sb(es, "b1_A8", [128, 16, 48], F32)
        win = sb(es, "b1_win", [128, 16, 32], F32)
        b_A2, b_A4, b_A8, b_win = Buf(), Buf(), Buf(), Buf()
        dbf = sb(es, "b1_d", [128, 4, 512], BF16)
        b_d = Buf()
        pm1 = sb(es, "b1_pm", [128, 4, 512], BF16)
        pms = [pm1, pm1]
        b_pm1 = Buf()
        b_pms = [b_pm1, b_pm1]
        mskt = sb(es, "b1_msk", [128, 768], F32)
        b_msk = Buf()
        icnt = sb(es, "b1_icn", [128, 4, 512], F32)
        b_icn = Buf()
        zft = sb(es, "b1_zf", [128, 4, 512], BF16)
        b_zft = Buf()
        gfs = sb(es, "b1_gf", [128, 512], F32)
        gps = sb(es, "b1_gp", [128, 512], F32)
        t1 = sb(es, "b1_t1", [128, 512], F32)
        t2 = sb(es, "b1_t2", [128, 512], F32)
        b_gf, b_gp, b_t1, b_t2 = Buf(), Buf(), Buf(), Buf()
        mg = sb(es, "b1_mg", [128, 8, 512], BF16)
        b_mg = Buf()
        xres = sb(es, "b1_xres", [128, D], F32)
        tmp = sb(es, "b1_tmp", [128, D], F32)
        b_xres, b_tmp = Buf(), Buf()
        ssm = sb(es, "b1_ssm", [128, 8], F32)
        b_ssm, b_wkb = Buf(), Buf()
        pT = ps(es, "b1_pT", [128, D], BF16)
        b_pT = Buf()
        pUM = ps(es, "b1_pUM", [128, 1024], F32)
        b_pUMh = [Buf(), Buf()]
        pG = [ps(es, "b1_pG%d" % i, [128, 512], F32) for i in range(4)]
        b_pG = [Buf() for _ in range(4)]

        fe_ctr = [0]
        def b1_fea(mt, i):
            tcn = fe_ctr[0]
            fe_ctr[0] += 1
            ix, i3 = tcn % NX, tcn % NXN
            r0 = 768 * mt + 128 * i
            xt, bx = xts[ix], b_xts[ix]
            S.dma("sp", xt[:], xb[r0:r0 + 128, :], writes=[bx])
            sst = ssr[i3]
            S.op("act", lambda e: e.activation(out=junk[:], in_=xt[:], func=AF.Square, accum_out=sst[:, 0:1]),
                 reads=[bx], writes=[b_ssr[i3]])
            dve_rstd(sst[:, 4:8], b_wkf[i3], [sst[:, 0:1]], b_ssr[i3], sst[:, 1:2], b_rsr[i3])
            xn = xns[i3]
            S.op("pool", lambda e: e.tensor_tensor(out=xn[:], in0=xt[:], in1=sst[:, 1:2].to_broadcast([128, D]),
                                                   op=ALU.mult), reads=[bx, b_rsr[i3]], writes=[b_xns[i3]])
            return i3

        def b1_feb(mt, i, i3):
            hT = hTs[mt % 2]
            bi = 0 if mt < 8 else 1
            transposes(xns[i3], b_xns[i3], pT, b_pT)
            evacs(pT, b_pT, (lambda k: hT[:, k, 128 * i:128 * i + 128]), b_hTs[mt % 2], bi, 0)

        def b1_up(mt, g, alt=False):
            hT = hTs[mt % 2]
            if alt:
                banks = [(pG[0][:, 0:384], b_pG[0]), (pG[1][:, 0:384], b_pG[1])]
            else:
                banks = [(pUM[:, 0:384], b_pUMh[0]), (pUM[:, 512:896], b_pUMh[1])]
            if g == 0:
                S.dma("sp", mskt[:], msk_d[:, 768 * mt:768 * mt + 768], writes=[b_msk])
                S.dma("sp", icnt[:], icn_d[:, :, 512 * mt:512 * mt + 512], writes=[b_icn])
            for h in range(2):
                for k in range(8):
                    S.op("pe", lambda e, h=h, k=k: e.matmul(
                        banks[h][0], lhsT=wi[:, k, 128 * g:128 * g + 128],
                        rhs=hT[:, k, 384 * h:384 * h + 384], start=(k == 0), stop=(k == 7)),
                        reads=[b_w, b_hTs[mt % 2]], writes=[banks[h][1]], partial=True)
            for h in range(2):
                S.op("dve", lambda e, h=h: e.tensor_tensor(
                    out=up[:, g, 384 * h:384 * h + 384], in0=banks[h][0],
                    in1=mskt[:, 384 * h:384 * h + 384], op=ALU.mult),
                    reads=[banks[h][1], b_msk], writes=[b_up], partial=True)

        def b1_pool(mt, g):
            U = up[:, g, :].rearrange("p (r c) -> p r c", c=48)

            def tt(out, a, b, rd, wr, op=ALU.add):
                S.op("pool", lambda e: e.tensor_tensor(out=out, in0=a, in1=b, op=op), reads=rd, writes=wr)
            if g == 0:
                tt(win[:], U[:, :, 7:39], U[:, :, 8:40], [b_up], [b_win])
            else:
                tt(A2[:, :, 0:47], U[:, :, 0:47], U[:, :, 1:48], [b_up], [b_A2])
                if g == 1:
                    tt(win[:], A2[:, :, 6:38], A2[:, :, 8:40], [b_A2], [b_win])
                else:
                    tt(A4[:, :, 0:45], A2[:, :, 0:45], A2[:, :, 2:47], [b_A2], [b_A4])
                    if g == 2:
                        tt(win[:], A4[:, :, 4:36], A4[:, :, 8:40], [b_A4], [b_win])
                    else:
                        tt(A8[:, :, 0:41], A4[:, :, 0:41], A4[:, :, 4:45], [b_A4], [b_A8])
                        tt(win[:], A8[:, :, 0:32], A8[:, :, 8:40], [b_A8], [b_win])
            tt(win[:], win[:], icnt[:, g, :].rearrange("p (r c) -> p r c", c=32), [b_win, b_icn], [b_win], op=ALU.mult)
            tt(dbf[:, g, :].rearrange("p (r c) -> p r c", c=32), win[:], U[:, :, 8:40], [b_win, b_up], [b_d],
               op=ALU.subtract)

        def b1_pg(mt):
            pm = pms[mt % 2]
            for g in range(4):
                ob, bb = pUM[:, 512 * (g % 2):512 * (g % 2) + 512], b_pUMh[g % 2]
                S.op("pe", lambda e, g=g, ob=ob: e.matmul(ob, lhsT=wpg[:, g, :], rhs=dbf[:, g, :], start=True, stop=True),
                     reads=[b_w, b_d], writes=[bb], partial=True)
                S.op("dve", lambda e, g=g, ob=ob: e.tensor_scalar(out=pm[:, g, :], in0=ob, scalar1=colv[:, 64 + g:65 + g],
                                                                  scalar2=None, op0=ALU.mult),
                     reads=[bb, b_colv], writes=[b_pms[mt % 2]], partial=True)

        def b1_gc(mt, c):
            hT, pm = hTs[mt % 2], pms[mt % 2]
            if c == 0:
                S.dma("sp", zft[:], zf_d[:, :, 512 * mt:512 * mt + 512].rearrange("j p t -> p j t"),
                      reads=[b_zf], writes=[b_zft])
            for pb, col0 in ((0, 512 + 128 * c), (1, 1536 + 128 * c)):
                for k in range(8):
                    S.op("pe", lambda e, pb=pb, col0=col0, k=k: e.matmul(
                        pG[pb][:].rearrange("p (r c) -> p r c", c=32), lhsT=wi[:, k, col0:col0 + 128],
                        rhs=hT[:, k, :].rearrange("p (r c) -> p r c", c=48)[:, :, 8:40],
                        start=(k == 0), stop=(k == 7)),
                        reads=[b_w, b_hTs[mt % 2]], writes=[b_pG[pb]], partial=True)
                gdst, b_g = (gfs, b_gf) if pb == 0 else (gps, b_gp)
                S.op("act", lambda e, gdst=gdst, pb=pb: e.activation(out=gdst[:], in_=pG[pb][:], func=AF.Sigmoid),
                     reads=[b_pG[pb]], writes=[b_g])
            for k in range(4):
                S.op("pe", lambda e, k=k: e.matmul(pG[3][:], lhsT=wpo[:, k, 128 * c:128 * c + 128],
                                                   rhs=pm[:, k, :], start=(k == 0), stop=(k == 3)),
                     reads=[b_w, b_pms[mt % 2]], writes=[b_pG[3]], partial=True)
            for k in range(4):
                S.op("pe", lambda e, k=k: e.matmul(pG[2][:], lhsT=wfo[:, k, 128 * c:128 * c + 128],
                                                   rhs=zft[:, k, :], start=(k == 0), stop=(k == 3)),
                     reads=[b_w, b_zft], writes=[b_pG[2]], partial=True)
            S.op("dve", lambda e: e.tensor_tensor(out=t1[:], in0=pG[2][:], in1=gfs[:], op=ALU.mult),
                 reads=[b_pG[2], b_gf], writes=[b_t1])
            S.op("dve", lambda e: e.tensor_tensor(out=t2[:], in0=pG[3][:], in1=gps[:], op=ALU.mult),
                 reads=[b_pG[3], b_gp], writes=[b_t2])
            S.op("pool", lambda e: e.tensor_tensor(out=mg[:, c, :], in0=t1[:], in1=t2[:], op=ALU.add),
                 reads=[b_t1, b_t2], writes=[b_mg], partial=True)

        def b1_wo(mt, s):
            bi = 0 if mt < 8 else 1
            if s % 2 == 0:
                zh, bz = [pUM[:, 0:512], pUM[:, 512:1024]], b_pUMh
            else:
                zh, bz = [pG[2][:], pG[3][:]], [b_pG[2], b_pG[3]]
            for n in range(2):
                for c in range(8):
                    S.op("pe", lambda e, n=n, c=c: e.matmul(
                        zh[n], lhsT=mg[:, c, 128 * s:128 * s + 128], rhs=wou[:, c, 512 * n:512 * n + 512],
                        start=(c == 0), stop=(c == 7)),
                        reads=[b_w, b_mg], writes=[bz[n]], partial=True)
            for rr in range(4):
                run = 16 * mt + 4 * s + rr
                S.dma("sp", xres[32 * rr:32 * rr + 32, :], xb[48 * run + 8:48 * run + 40, :],
                      writes=[b_xres], partial=True)
            r0 = 512 * mt + 128 * s
            backend(zh, bz, junk, ssm, b_ssm, ssm[:, 4:8], b_wkb, rowg1[:, bi, :], tmp, b_tmp, xres, b_xres,
                    x1_d[r0:r0 + 128, :], b_x1)

        def b1_fe_sched(mt):
            slots = [[] for _ in range(8)]
            ids = {}

            def mk_a(i):
                return lambda: ids.__setitem__(i, b1_fea(mt, i))

            def mk_b(i):
                return lambda: b1_feb(mt, i, ids[i])
            plan = {0: [("a", 0), ("a", 1)], 1: [("b", 0), ("a", 2)], 2: [("b", 1), ("a", 3)], 3: [("b", 2), ("a", 4)],
                    4: [("b", 3), ("a", 5)], 5: [("b", 4)], 6: [("b", 5)]}
            for c, lst in plan.items():
                for kind, i in lst:
                    slots[c].append(mk_a(i) if kind == "a" else mk_b(i))
            return slots

        for slot in b1_fe_sched(0):
            for f in slot:
                f()
        for g in range(4):
            b1_up(0, g)
            b1_pool(0, g)
        for mt in range(NMT):
            b1_pg(mt)
            nxt = b1_fe_sched(mt + 1) if mt + 1 < NMT else [[] for _ in range(8)]
            for c in range(8):
                b1_gc(mt, c)
                for f in nxt[c]:
                    f()
                if mt + 1 < NMT and c >= 6:
                    b1_up(mt + 1, c - 6)
                    b1_pool(mt + 1, c - 6)
            for s in range(4):
                b1_wo(mt, s)
                if mt + 1 < NMT and s < 2:
                    b1_up(mt + 1, 2 + s, alt=True)
                    b1_pool(mt + 1, 2 + s)
        S.barrier()
        S.emit()
    mid.close()

    with ExitStack() as es:
        wg = sb(es, "b2_wg", [128, 8, DFF], BF16)
        wu = sb(es, "b2_wu", [128, 8, DFF], BF16)
        wd = sb(es, "b2_wd", [128, NJ, D], BF16)
        b_w = Buf()
        for c0 in range(0, DFF, 704):
            S.dma("sp", wg[:, :, c0:c0 + 704], wg_bf[:, c0:c0 + 704].rearrange("(k p) c -> p k c", p=128),
                  reads=[b_wscr], writes=[b_w], partial=True)
            S.dma("sp", wu[:, :, c0:c0 + 704], wu_bf[:, c0:c0 + 704].rearrange("(k p) c -> p k c", p=128),
                  reads=[b_wscr], writes=[b_w], partial=True)
        for j0 in range(0, NJ, 6):
            nj = min(6, NJ - j0)
            S.dma("sp", wd[:, j0:j0 + nj, :], wd_bf[128 * j0:128 * (j0 + nj), :].rearrange("(k p) c -> p k c", p=128),
                  reads=[b_wscr], writes=[b_w], partial=True)
        S.barrier()
        S.emit()
        NX = 2
        xts = [sb(es, "b2_xt%d" % i, [128, D], F32) for i in range(NX)]
        b_xts = [Buf() for _ in range(NX)]
        junk = sb(es, "b2_junk", [128, D], BF16)
        xns = [sb(es, "b2_xn%d" % i, [128, D], BF16) for i in range(2)]
        b_xns = [Buf(), Buf()]
        ssr = [sb(es, "b2_ss%d" % i, [128, 8], F32) for i in range(2)]
        b_ssr = [Buf(), Buf()]
        b_rsr = [Buf(), Buf()]
        b_wkf = [Buf(), Buf()]
        hTs = [sb(es, "b2_hT%d" % i, [128, 8, 512], BF16) for i in range(2)]
        b_hTs = [Buf(), Buf()]
        sa = [sb(es, "b2_sa%d" % i, [128, 512], F32) for i in range(2)]
        b_sa = [Buf(), Buf()]
        aT = sb(es, "b2_aT", [128, NJ, 512], BF16)
        b_aT = Buf()
        xres = sb(es, "b2_xres", [128, D], F32)
        tmp = sb(es, "b2_tmp", [128, D], F32)
        b_xres, b_tmp = Buf(), Buf()
        ssm = sb(es, "b2_ssm", [128, 8], F32)
        b_ssm, b_wkb = Buf(), Buf()
        b_y = Buf()
        pT = ps(es, "b2_pT", [128, D], BF16)
        b_pT = Buf()
        pA = [ps(es, "b2_pA%d" % i, [128, 512], F32) for i in range(2)]
        pB = [ps(es, "b2_pB%d" % i, [128, 512], F32) for i in range(2)]
        b_pA = [Buf(), Buf()]
        b_pB = [Buf(), Buf()]
        pZ = ps(es, "b2_pZ", [128, 1024], F32)
        b_pZh = [Buf(), Buf()]
        fe_ctr = [0]

        def b2_fea(mt, s):
            tcn = fe_ctr[0]
            fe_ctr[0] += 1
            ix, i2 = tcn % NX, tcn % 2
            r0 = 512 * mt + 128 * s
            xt, bx = xts[ix], b_xts[ix]
            S.dma("sp", xt[:], x1_d[r0:r0 + 128, :], reads=[b_x1], writes=[bx])
            sst = ssr[i2]
            S.op("act", lambda e: e.activation(out=junk[:], in_=xt[:], func=AF.Square, accum_out=sst[:, 0:1]),
                 reads=[bx], writes=[b_ssr[i2]])
            dve_rstd(sst[:, 4:8], b_wkf[i2], [sst[:, 0:1]], b_ssr[i2], sst[:, 1:2], b_rsr[i2])
            xn = xns[i2]
            S.op("pool", lambda e: e.tensor_tensor(out=xn[:], in0=xt[:], in1=sst[:, 1:2].to_broadcast([128, D]),
                                                   op=ALU.mult), reads=[bx, b_rsr[i2]], writes=[b_xns[i2]])
            return i2

        def b2_feb(mt, s, i2):
            hT = hTs[mt % 2]
            bi = 0 if mt < 8 else 1
            transposes(xns[i2], b_xns[i2], pT, b_pT)
            evacs(pT, b_pT, (lambda k: hT[:, k, 128 * s:128 * s + 128]), b_hTs[mt % 2], bi, 1)

        def b2_fe_sched(mt):
            slots = {}
            ids = {}

            def mk_a(i):
                return lambda: ids.__setitem__(i, b2_fea(mt, i))

            def mk_b(i):
                return lambda: b2_feb(mt, i, ids[i])
            plan = {1: [("a", 0)], 4: [("b", 0), ("a", 1)], 8: [("b", 1), ("a", 2)], 12: [("b", 2), ("a", 3)],
                    16: [("b", 3)]}
            for j, lst in plan.items():
                slots[j] = [mk_a(i) if kind == "a" else mk_b(i) for kind, i in lst]
            return slots

        def b2_gu(mt, j):
            hT = hTs[mt % 2]
            i2 = j % 2
            for k in range(8):
                S.op("pe", lambda e, k=k: e.matmul(pA[i2][:], lhsT=wg[:, k, 128 * j:128 * j + 128],
                                                   rhs=hT[:, k, :], start=(k == 0), stop=(k == 7)),
                     reads=[b_w, b_hTs[mt % 2]], writes=[b_pA[i2]], partial=True)
            for k in range(8):
                S.op("pe", lambda e, k=k: e.matmul(pB[i2][:], lhsT=wu[:, k, 128 * j:128 * j + 128],
                                                   rhs=hT[:, k, :], start=(k == 0), stop=(k == 7)),
                     reads=[b_w, b_hTs[mt % 2]], writes=[b_pB[i2]], partial=True)
            S.op("act", lambda e: e.activation(out=sa[i2][:], in_=pA[i2][:], func=AF.Silu),
                 reads=[b_pA[i2]], writes=[b_sa[i2]])
            S.op("dve", lambda e: e.tensor_tensor(out=aT[:, j, :], in0=pB[i2][:], in1=sa[i2][:], op=ALU.mult),
                 reads=[b_pB[i2], b_sa[i2]], writes=[b_aT], partial=True)

        def b2_dn(mt, s):
            bi = 0 if mt < 8 else 1
            if s % 2 == 0:
                zh, bz = [pZ[:, 0:512], pZ[:, 512:1024]], b_pZh
            else:
                zh, bz = [pA[0][:], pB[0][:]], [b_pA[0], b_pB[0]]
            for n in range(2):
                for j in range(NJ):
                    S.op("pe", lambda e, n=n, j=j: e.matmul(
                        zh[n], lhsT=aT[:, j, 128 * s:128 * s + 128], rhs=wd[:, j, 512 * n:512 * n + 512],
                        start=(j == 0), stop=(j == NJ - 1)),
                        reads=[b_w, b_aT], writes=[bz[n]], partial=True)
            r0 = 512 * mt + 128 * s
            S.dma("sp", xres[:], x1_d[r0:r0 + 128, :], reads=[b_x1], writes=[b_xres])
            out_toks.append(backend(zh, bz, junk, ssm, b_ssm, ssm[:, 4:8], b_wkb, rowg2[:, bi, :], tmp, b_tmp,
                                    xres, b_xres, y_d[r0:r0 + 128, :], b_y))

        sl = b2_fe_sched(0)
        for j in sorted(sl):
            for f in sl[j]:
                f()
        for mt in range(NMT):
            nxt = b2_fe_sched(mt + 1) if mt + 1 < NMT else {}
            for j in range(NJ):
                b2_gu(mt, j)
                for f in nxt.get(j, []):
                    f()
            for s in range(4):
                b2_dn(mt, s)
        S.barrier()
        S.emit()
    top.close()
    return nc


_CACHE = {}


def _core_inputs(core, x_prompt, x_sample, c_prompt, c_sample, w, shared):
    b, q = core // 4, core % 4
    xa = np.concatenate([x_prompt[b], x_sample[core]], axis=0)
    xb = np.zeros((NRUN, 48, D), np.float32)
    msk = np.zeros((NRUN, 48), np.float32)
    for r, (sid, start, S) in enumerate(run_starts(q)):
        src = x_prompt[b] if sid == 0 else x_sample[core]
        lo, hi = start - 8, start + 40
        l2, h2 = max(lo, 0), min(hi, S)
        xb[r, l2 - lo:h2 - lo] = src[l2:h2]
        msk[r, l2 - lo:h2 - lo] = 1.0
    colv = np.zeros((128, 70), np.float32)

    def col(v):
        return np.ascontiguousarray(v.reshape(-1, 128).T)
    colv[:, 0:8] = col(c_prompt[b])
    colv[:, 8:16] = col(c_sample[core])
    colv[:, 16:24] = col(w["g_pre_mix"])
    colv[:, 24:32] = col(w["g_pre_ffn"])
    ba = w["b_ada"]
    colv[:, 32:40] = col(ba[0:D])
    colv[:, 40:48] = col(ba[D:2 * D])
    colv[:, 48:56] = col(ba[3 * D:4 * D])
    colv[:, 56:64] = col(ba[4 * D:5 * D])
    colv[:, 64:68] = col(w["pool_scale"])
    colv[:, 68] = 1.0 if q == 0 else 0.0
    colv[:, 69] = 1.0
    rowv = np.stack([w["g_post_mix"], w["g_post_ffn"], ba[2 * D:3 * D], ba[5 * D:6 * D]], axis=0)
    rowv = np.ascontiguousarray(np.broadcast_to(rowv[None], (128, 4, D)))
    E_p, T_p, E_s, T_s, CS = shared["four"][q]
    m = {
        "xa": np.ascontiguousarray(xa), "xb": xb.reshape(NRUN * 48, D),
        "msk": np.ascontiguousarray(np.broadcast_to(msk.reshape(1, -1), (128, NRUN * 48))),
        "icn": np.ascontiguousarray(np.broadcast_to(shared["icn"][q][None], (128, 4, NOWN))),
        "colv": colv, "rowv": rowv, "ident": shared["ident"],
        "Ep": E_p, "Tp": T_p, "Es": E_s, "Ts": T_s, "CS": CS,
        "w_ada": w["w_ada"], "w_in": w["w_in"], "w_fo": w["w_fo"], "w_pg": w["w_pg"], "w_po": w["w_po"],
        "w_out": w["w_out"], "w_gate": w["w_gate"], "w_up": w["w_up"], "w_down": w["w_down"],
    }
    return m


def kernel(x_prompt, x_sample, c_prompt, c_sample, w_ada, b_ada, g_pre_mix, w_in, w_fo, w_pg, pool_scale, w_po,
           w_out, g_post_mix, g_pre_ffn, w_gate, w_up, w_down, g_post_ffn):
    f = lambda a: np.ascontiguousarray(np.asarray(a, dtype=np.float32))
    x_prompt, x_sample, c_prompt, c_sample = f(x_prompt), f(x_sample), f(c_prompt), f(c_sample)
    w = {"w_ada": f(w_ada)[0], "b_ada": f(b_ada)[0], "g_pre_mix": f(g_pre_mix)[0], "w_in": f(w_in)[0],
         "w_fo": f(w_fo)[0], "w_pg": f(w_pg)[0], "pool_scale": f(pool_scale)[0], "w_po": f(w_po)[0],
         "w_out": f(w_out)[0], "g_post_mix": f(g_post_mix)[0], "g_pre_ffn": f(g_pre_ffn)[0],
         "w_gate": f(w_gate)[0], "w_up": f(w_up)[0], "w_down": f(w_down)[0], "g_post_ffn": f(g_post_ffn)[0]}
    if "shared" not in _CACHE:
        _CACHE["shared"] = {
            "four": [fourier_consts(q) for q in range(4)],
            "icn": [inv_counts(q) for q in range(4)],
            "ident": np.eye(128, dtype=np.float32).astype(BF),
        }
    shared = _CACHE["shared"]
    if "nc" not in _CACHE:
        _CACHE["nc"] = build_program()
    nc = _CACHE["nc"]
    in_maps = [_core_inputs(c, x_prompt, x_sample, c_prompt, c_sample, w, shared) for c in range(8)]
    res = run_bass_kernel_spmd(nc, in_maps, core_ids=list(range(8)))
    y_prompt = np.empty((2, SP, D), np.float32)
    y_sample = np.empty((8, SS, D), np.float32)
    for c in range(8):
        b, q = c // 4, c % 4
        y = np.asarray(res.results[c]["y"], dtype=np.float32)
        yp = y[:4096].reshape(128, 32, D)
        y_prompt[b].reshape(128, 128, D)[:, 32 * q:32 * q + 32, :] = yp
        y_sample[c] = y[4096:]
    return (y_prompt, y_sample)
```
